# Optimizing a Trainium2 kernel written in Bass

```python
import jax, jax.numpy as jnp
from jax import lax
import numpy as np

D_MODEL = 1024
BATCH = 16
SEQ = 2048
DEPTH = 1
DEC_BATCH = 8
DEC_SEQ = 32
PAST_LEN = 1024

CHUNK = 64
BAND_CHUNKS = 8
ATTN_PAST = BAND_CHUNKS * CHUNK
D_MIX = D_MODEL
SSD_WIDTH = D_MIX // 2
SSD_HEAD_DIM = 64
N_SSD_HEADS = SSD_WIDTH // SSD_HEAD_DIM
N_SSD_GROUPS = 2
HEADS_PER_GROUP = N_SSD_HEADS // N_SSD_GROUPS
D_STATE = 128
D_CONV = 4
CONV_DIM = SSD_WIDTH + 2 * N_SSD_GROUPS * D_STATE
ATTN_WIDTH = D_MIX - SSD_WIDTH
ATTN_HEAD_DIM = 64
N_ATTN_HEADS = ATTN_WIDTH // ATTN_HEAD_DIM
MAX_REL = 128
D_FF = -(-8 * D_MODEL // (3 * 256)) * 256
IN_SIZES = [SSD_WIDTH, CONV_DIM, N_SSD_HEADS, ATTN_WIDTH, ATTN_WIDTH, ATTN_WIDTH]
IN_PROJ = sum(IN_SIZES)
IN_SPLITS = [int(s) for s in np.cumsum(IN_SIZES)[:-1]]
EPS = 1e-6
NEG = -1e30

kernel_name = "hymba_ssd_chunkband_stream_step"


def rmsnorm(x, g):
    xf = x.astype(jnp.float32)
    y = xf * lax.rsqrt(jnp.mean(xf * xf, axis=-1, keepdims=True) + EPS)
    return (y * g.astype(jnp.float32)).astype(x.dtype)


def causal_conv(xbc, conv_state, w, b):
    l = xbc.shape[1]
    xpad = jnp.concatenate([conv_state.astype(xbc.dtype), xbc], axis=1)
    out = b
    for tap in range(D_CONV):
        out = out + xpad[:, tap:tap + l] * w[tap]
    return out, xpad[:, -(D_CONV - 1):]


def ssd_scan(x, dt, a, bm, cm, h0, chunk_len):
    bsz, l, nh, p = x.shape
    nc = l // chunk_len
    r = lambda t: t.reshape((bsz, nc, chunk_len) + t.shape[2:])
    x, dt, bm, cm = r(x), r(dt), r(bm), r(cm)
    da_cum = jnp.cumsum(dt * a, axis=2)
    seg = da_cum[:, :, :, None, :] - da_cum[:, :, None, :, :]
    causal = jnp.tril(jnp.ones((chunk_len, chunk_len), bool))[None, None, :, :, None]
    decay = jnp.where(causal, jnp.exp(jnp.where(causal, seg, 0.0)), 0.0)
    xdt = x * dt[..., None]
    scores = jnp.einsum('bcqhn,bcshn->bcqsh', cm, bm) * decay
    y_diag = jnp.einsum('bcqsh,bcshp->bcqhp', scores, xdt)
    decay_to_end = jnp.exp(da_cum[:, :, -1:, :] - da_cum)
    chunk_states = jnp.einsum('bcqhn,bcqh,bcqhp->bchpn', bm, decay_to_end, xdt)
    chunk_decay = jnp.exp(da_cum[:, :, -1, :])

    def step(h, inp):
        s, d = inp
        return d[:, :, None, None] * h + s, h

    h_final, h_prev = lax.scan(step, h0.astype(x.dtype),
                               (jnp.moveaxis(chunk_states, 1, 0), jnp.moveaxis(chunk_decay, 1, 0)))
    h_prev = jnp.moveaxis(h_prev, 0, 1)
    y_off = jnp.einsum('bcqhn,bchpn,bcqh->bcqhp', cm, h_prev, jnp.exp(da_cum))
    return (y_diag + y_off).reshape(bsz, l, nh, p), h_final


def ssd_group(z, xbc, dt_raw, conv_state, h0, chunk_len, conv_w, conv_b, dt_bias, a_log, d_skip, ssd_norm_g):
    xbc, new_conv = causal_conv(xbc, conv_state, conv_w, conv_b)
    xbc = jax.nn.silu(xbc)
    xs, bm, cm = jnp.split(xbc, [SSD_WIDTH, SSD_WIDTH + N_SSD_GROUPS * D_STATE], axis=-1)
    bsz, l, _ = xs.shape
    xs = xs.reshape(bsz, l, N_SSD_HEADS, SSD_HEAD_DIM)
    bm = jnp.repeat(bm.reshape(bsz, l, N_SSD_GROUPS, D_STATE), HEADS_PER_GROUP, axis=2)
    cm = jnp.repeat(cm.reshape(bsz, l, N_SSD_GROUPS, D_STATE), HEADS_PER_GROUP, axis=2)
    dt = jax.nn.softplus(dt_raw + dt_bias)
    a = -jnp.exp(a_log)
    y, h_final = ssd_scan(xs, dt, a, bm, cm, h0, chunk_len)
    y = (y + d_skip[:, None] * xs).reshape(bsz, l, SSD_WIDTH)
    y = rmsnorm(y * jax.nn.silu(z), ssd_norm_g)
    return y, new_conv, h_final


def rel_bias_lookup(rel_table, rel):
    return rel_table[:, jnp.clip(rel, -MAX_REL, MAX_REL) + MAX_REL].astype(jnp.float32)


def band_attention_prompt(q, k, v, rel_table):
    bsz, l, nh, hd = q.shape
    nc = l // CHUNK
    band = ATTN_PAST + CHUNK
    pad = ((0, 0), (ATTN_PAST, 0), (0, 0), (0, 0))
    kp, vp = jnp.pad(k, pad), jnp.pad(v, pad)
    idx = jnp.arange(nc)[:, None] * CHUNK + jnp.arange(band)[None, :]
    kb, vb = kp[:, idx], vp[:, idx]
    qc = q.reshape(bsz, nc, CHUNK, nh, hd)
    rel = jnp.arange(CHUNK)[:, None] + ATTN_PAST - jnp.arange(band)[None, :]
    bias = rel_bias_lookup(rel_table, rel)
    valid = idx >= ATTN_PAST
    s = jnp.einsum('bcqhd,bckhd->bchqk', qc, kb).astype(jnp.float32) * (ATTN_HEAD_DIM ** -0.5) + bias[None, None]
    s = jnp.where(valid[None, :, None, None, :], s, NEG)
    pr = jax.nn.softmax(s, axis=-1).astype(v.dtype)
    o = jnp.einsum('bchqk,bckhd->bcqhd', pr, vb)
    return o.reshape(bsz, l, nh * hd)


def attention_sample(q, k, v, cache_k, cache_v, rel_table):
    bsz, l, nh, hd = q.shape
    cr = cache_k.shape[1]
    kk = jnp.concatenate([cache_k.astype(k.dtype), k], axis=1)
    vv = jnp.concatenate([cache_v.astype(v.dtype), v], axis=1)
    kpos = jnp.concatenate([jnp.arange(cr) - cr, jnp.arange(l)])
    rel = jnp.arange(l)[:, None] - kpos[None, :]
    bias = rel_bias_lookup(rel_table, rel)
    s = jnp.einsum('bqhd,bkhd->bhqk', q, kk).astype(jnp.float32) * (ATTN_HEAD_DIM ** -0.5) + bias[None]
    pr = jax.nn.softmax(s, axis=-1).astype(v.dtype)
    o = jnp.einsum('bhqk,bkhd->bqhd', pr, vv)
    return o.reshape(bsz, l, nh * hd)


def layer(x, conv_state, ssm_state, cache_k, cache_v, chunk_len,
          norm_mix_g, w_in, conv_w, conv_b, dt_bias, a_log, d_skip, ssd_norm_g,
          q_norm_g, k_norm_g, rel_bias, w_out, norm_ffn_g, w_gate, w_up, w_down):
    bsz, l, _ = x.shape
    h = rmsnorm(x, norm_mix_g)
    u = h @ w_in
    z, xbc, dt_raw, q, k, v = jnp.split(u, IN_SPLITS, axis=-1)
    y_ssd, new_conv, new_ssm = ssd_group(z, xbc, dt_raw, conv_state, ssm_state, chunk_len,
                                         conv_w, conv_b, dt_bias, a_log, d_skip, ssd_norm_g)
    q = rmsnorm(q.reshape(bsz, l, N_ATTN_HEADS, ATTN_HEAD_DIM), q_norm_g)
    k = rmsnorm(k.reshape(bsz, l, N_ATTN_HEADS, ATTN_HEAD_DIM), k_norm_g)
    v = v.reshape(bsz, l, N_ATTN_HEADS, ATTN_HEAD_DIM)
    if cache_k is None:
        o = band_attention_prompt(q, k, v, rel_bias)
        keep = min(ATTN_PAST, l)
        new_k, new_v = k[:, l - keep:], v[:, l - keep:]
    else:
        o = attention_sample(q, k, v, cache_k, cache_v, rel_bias)
        new_k, new_v = k, v
    x = x + jnp.concatenate([y_ssd, o], axis=-1) @ w_out
    f = rmsnorm(x, norm_ffn_g)
    x = x + (jax.nn.silu(f @ w_gate) * (f @ w_up)) @ w_down
    return x, new_k, new_v, new_ssm, new_conv


def setup_inputs(seed: int = 0) -> dict:
    key = jax.random.key(seed)
    ks = jax.random.split(key, 24)
    f32 = jnp.float32
    cache_rows = min(ATTN_PAST, PAST_LEN)
    nrm = lambda k, shape, scale: jax.random.normal(k, shape, f32) * scale
    dt0 = jnp.exp(jax.random.uniform(ks[10], (DEPTH, N_SSD_HEADS), f32, np.log(1e-3), np.log(1e-1)))
    return {
        "x_prompt": nrm(ks[0], (BATCH, SEQ, D_MODEL), 1.0),
        "x_sample": nrm(ks[1], (DEC_BATCH, DEC_SEQ, D_MODEL), 1.0),
        "cache_attn_k": nrm(ks[2], (DEPTH, DEC_BATCH, cache_rows, N_ATTN_HEADS, ATTN_HEAD_DIM), 1.0),
        "cache_attn_v": nrm(ks[3], (DEPTH, DEC_BATCH, cache_rows, N_ATTN_HEADS, ATTN_HEAD_DIM), 1.0),
        "state_ssm": nrm(ks[4], (DEPTH, DEC_BATCH, N_SSD_HEADS, SSD_HEAD_DIM, D_STATE), 0.1),
        "state_conv": nrm(ks[5], (DEPTH, DEC_BATCH, D_CONV - 1, CONV_DIM), 1.0),
        "norm_mix_g": 1.0 + nrm(ks[6], (DEPTH, D_MODEL), 0.02),
        "w_in": nrm(ks[7], (DEPTH, D_MODEL, IN_PROJ), D_MODEL ** -0.5),
        "conv_w": nrm(ks[8], (DEPTH, D_CONV, CONV_DIM), D_CONV ** -0.5),
        "conv_b": nrm(ks[9], (DEPTH, CONV_DIM), 0.02),
        "dt_bias": dt0 + jnp.log(-jnp.expm1(-dt0)),
        "a_log": jnp.log(jax.random.uniform(ks[11], (DEPTH, N_SSD_HEADS), f32, 1.0, 16.0)),
        "d_skip": 1.0 + nrm(ks[12], (DEPTH, N_SSD_HEADS), 0.1),
        "ssd_norm_g": 1.0 + nrm(ks[13], (DEPTH, SSD_WIDTH), 0.02),
        "q_norm_g": 1.0 + nrm(ks[14], (DEPTH, ATTN_HEAD_DIM), 0.02),
        "k_norm_g": 1.0 + nrm(ks[15], (DEPTH, ATTN_HEAD_DIM), 0.02),
        "rel_bias": nrm(ks[16], (DEPTH, N_ATTN_HEADS, 2 * MAX_REL + 1), 0.1),
        "w_out": nrm(ks[17], (DEPTH, D_MIX, D_MODEL), D_MIX ** -0.5),
        "norm_ffn_g": 1.0 + nrm(ks[18], (DEPTH, D_MODEL), 0.02),
        "w_gate": nrm(ks[19], (DEPTH, D_MODEL, D_FF), D_MODEL ** -0.5),
        "w_up": nrm(ks[20], (DEPTH, D_MODEL, D_FF), D_MODEL ** -0.5),
        "w_down": nrm(ks[21], (DEPTH, D_FF, D_MODEL), D_FF ** -0.5),
    }


def reference(x_prompt, x_sample, cache_attn_k, cache_attn_v, state_ssm, state_conv,
              norm_mix_g, w_in, conv_w, conv_b, dt_bias, a_log, d_skip, ssd_norm_g,
              q_norm_g, k_norm_g, rel_bias, w_out, norm_ffn_g, w_gate, w_up, w_down):
    weights = [norm_mix_g, w_in, conv_w, conv_b, dt_bias, a_log, d_skip, ssd_norm_g,
               q_norm_g, k_norm_g, rel_bias, w_out, norm_ffn_g, w_gate, w_up, w_down]
    yp, ys = x_prompt, x_sample
    kp_l, vp_l, sp_l, cp_l, ks_l, vs_l, ss_l, cs_l = [], [], [], [], [], [], [], []
    for i in range(DEPTH):
        wl = [w[i] for w in weights]
        bp = yp.shape[0]
        zero_conv = jnp.zeros((bp, D_CONV - 1, CONV_DIM), yp.dtype)
        zero_ssm = jnp.zeros((bp, N_SSD_HEADS, SSD_HEAD_DIM, D_STATE), yp.dtype)
        yp, kp, vp, sp, cp = layer(yp, zero_conv, zero_ssm, None, None, CHUNK, *wl)
        ys, kn, vn, sn, cn = layer(ys, state_conv[i], state_ssm[i], cache_attn_k[i], cache_attn_v[i],
                                   ys.shape[1], *wl)
        kp_l.append(kp); vp_l.append(vp); sp_l.append(sp); cp_l.append(cp)
        ks_l.append(kn); vs_l.append(vn); ss_l.append(sn); cs_l.append(cn)
    k_prompt, v_prompt = jnp.stack(kp_l), jnp.stack(vp_l)
    ssm_prompt, conv_prompt = jnp.stack(sp_l), jnp.stack(cp_l)
    k_sample, v_sample = jnp.stack(ks_l), jnp.stack(vs_l)
    ssm_sample, conv_sample = jnp.stack(ss_l), jnp.stack(cs_l)
    return (yp, ys, k_prompt, v_prompt, ssm_prompt, conv_prompt, k_sample, v_sample, ssm_sample, conv_sample)
```

```python
import contextlib
import numpy as np
import concourse.bass as bass
import concourse.mybir as mybir
from concourse.bass_utils import run_bass_kernel_spmd
from concourse.ap import AP

F32 = mybir.dt.float32
BF16 = mybir.dt.bfloat16
AF = mybir.ActivationFunctionType
ALU = mybir.AluOpType

NCORES = 8
D = 1024
SEQ = 2048
NPROJ = 3080
DFF = 2816
NJB = 22
EPS = 1e-6
C_Z, C_XBC, C_DT, C_Q, C_K, C_V = 0, 512, 1536, 1544, 2056, 2568


class Buf:
    __slots__ = ("name", "w", "r", "excl")

    def __init__(self, name, excl=False):
        self.name = name
        self.w = None
        self.r = []
        self.excl = excl


class Prog:
    ENGS = ("pe", "act", "dve", "pool", "sp")

    def __init__(self, nc):
        self.nc = nc
        self.ops = {e: [] for e in self.ENGS}
        self.cnt = {e: 0 for e in self.ENGS}
        self.known = {e: {} for e in self.ENGS}
        self.same = {"act", "dve", "pool"}
        self.dma_keys = []
        self.sems = {}

    def dma_sem(self, key):
        self.cnt[key] = 0
        self.dma_keys.append(key)

    def _deps(self, eng, reads, writes):
        deps = {}

        def add(tok):
            if tok is None:
                return
            k, v = tok
            if k == eng and eng not in self.same:
                return
            if deps.get(k, 0) < v:
                deps[k] = v
        for b in reads:
            add(b.w)
            if b.excl:
                for t in b.r:
                    if t[0] != eng:
                        add(t)
        for b in writes:
            add(b.w)
            for t in b.r:
                if t[0] == eng:
                    continue
                add(t)
        waits = []
        kn = self.known[eng]
        for k, v in deps.items():
            if kn.get(k, 0) < v:
                kn[k] = v
                waits.append((k, v))
        return waits

    def _commit(self, tok, reads, writes):
        for b in reads:
            b.r.append(tok)
        for b in writes:
            b.w = tok
            b.r = []

    def op(self, eng, fn, reads=(), writes=()):
        waits = self._deps(eng, reads, writes)
        self.cnt[eng] += 1
        tok = (eng, self.cnt[eng])
        self.ops[eng].append((waits, fn, (eng, 1)))
        self._commit(tok, reads, writes)
        return tok

    def dma(self, q, out, in_, reads=(), writes=(), store=False, **kw):
        if writes and not store:
            key = "L_" + writes[0].name
        else:
            key = "S_" + reads[0].name
        if key not in self.cnt:
            self.cnt[key] = 0
            self.dma_keys.append(key)
        waits = self._deps(q, reads, writes)
        self.cnt[key] += 16
        tok = (key, self.cnt[key])

        def fn(e, out=out, in_=in_, kw=kw):
            return e.dma_start(out=out, in_=in_, **kw)
        self.ops[q].append((waits, fn, (key, 16)))
        self._commit(tok, reads, writes)
        return tok

    def barrier(self, skip_prefix=None):
        toks = [(k, v) for k, v in self.cnt.items() if v > 0 and not (skip_prefix and k.startswith(skip_prefix))]
        for e in self.ENGS:
            self.wait_all(e, toks)

    def wait_all(self, eng, toks):
        waits = []
        kn = self.known[eng]
        for k, v in toks:
            if kn.get(k, 0) < v:
                kn[k] = v
                waits.append((k, v))
        self.ops[eng].append((waits, None, None))

    def emit(self):
        nc = self.nc
        with contextlib.ExitStack() as st:
            for k in list(self.ENGS) + self.dma_keys:
                self.sems[k] = st.enter_context(nc.semaphore("s_" + k))
            block = st.enter_context(nc.Block())
            hooks = {"pe": block.tensor, "act": block.scalar, "dve": block.vector,
                     "pool": block.gpsimd, "sp": block.sync}
            for e in self.ENGS:
                ops = self.ops[e]

                def run(engine, ops=ops):
                    for waits, fn, inc in ops:
                        for k, v in waits:
                            engine.wait_ge(self.sems[k], v)
                        if fn is not None:
                            ins = fn(engine)
                            ins.then_inc(self.sems[inc[0]], inc[1])
                hooks[e](run)


class Rot:
    def __init__(self, items):
        self.items = items
        self.i = 0

    def next(self):
        it = self.items[self.i % len(self.items)]
        self.i += 1
        return it


def bc3(ap2, n):
    return ap2.unsqueeze(2).to_broadcast([ap2.shape[0], ap2.shape[1], n])


class _Stop(Exception):
    pass


DBG = {"stop": None, "dump": False, "names": [], "kv": "both"}


def _chk(name):
    if DBG["stop"] == name:
        raise _Stop()


def build_program():
    nc = bass.Bass("TRN2", target_bir_lowering=False)
    P = Prog(nc)

    def din(name, shape):
        return nc.dram_tensor(name, shape, F32, kind="ExternalInput")

    def dout(name, shape):
        return nc.dram_tensor(name, shape, F32, kind="ExternalOutput")

    xp = din("xp", [2, SEQ, D])
    xsm = din("xsm", [32, D])
    ck = din("ck", [512, 512])
    cv = din("cv", [512, 512])
    sssm = din("sssm", [512, 128])
    sconv = din("sconv", [3, D])
    w_in = din("w_in", [D, NPROJ])
    w_out = din("w_out", [D, D])
    w_gate = din("w_gate", [D, DFF])
    w_up = din("w_up", [D, DFF])
    w_down = din("w_down", [DFF, D])
    vecs = din("vecs", [7, D])
    small8 = din("small8", [3, 8])
    ssdg = din("ssdg", [1, 512])
    qkg = din("qkg", [2, 64])
    relb = din("relb", [8, 257])

    yp = dout("yp", [2, SEQ, D])
    ys = dout("ys", [32, D])
    kpo = dout("kpo", [2, 512, 512])
    vpo = dout("vpo", [2, 512, 512])
    spo = dout("spo", [2, 512, 128])
    cpo = dout("cpo", [2, 3, D])
    kso = dout("kso", [32, 512])
    vso = dout("vso", [32, 512])
    sso = dout("sso", [512, 128])
    cso = dout("cso", [3, D])

    text = nc.dram_tensor("text", [8, 128, 768], F32, kind="Internal")
    text1 = nc.dram_tensor("text1", [8, 768], F32, kind="Internal")
    B_text1 = Buf("text1")
    wbs = nc.dram_tensor("wbs", [NJB, 128, 3072], BF16, kind="Internal")
    B_text, B_wbs = Buf("text"), Buf("wbs")

    est = contextlib.ExitStack()
    with est:
        def sb(name, shape, dt):
            return est.enter_context(nc.sbuf_tensor(name, shape, dt)), Buf(name)

        win, B_win = sb("win", [128, 8, NPROJ], BF16)
        wout, B_wout = sb("wout", [128, 8, D], BF16)
        wgu = [sb("wgu%d" % i, [128, 2048], BF16) for i in range(3)]
        wdn = [sb("wdn%d" % i, [128, 1024], BF16) for i in range(3)]
        ident_bf, B_idb = sb("ident_bf", [128, 128], BF16)
        ident_f, B_idf = sb("ident_f", [128, 128], F32)
        tri_f, B_tri = sb("tri_f", [128, 128], F32)
        ones_f, B_ones = sb("ones_f", [128, 128], F32)
        maskneg, B_mneg = sb("maskneg", [128, 128], F32)
        blk_bf, B_blk = sb("blk_bf", [128, 128], BF16)
        vecT, B_vecT = sb("vecT", [128, 8, 7], F32)
        gq2, B_gq = sb("gq2", [128, 1], F32)
        gk2, B_gk = sb("gk2", [128, 1], F32)
        s8bc, B_s8 = sb("s8bc", [128, 3, 8], F32)
        ssdg_bc, B_ssdg = sb("ssdg_bc", [128, 512], F32)
        eps_c, B_eps = sb("eps_c", [128, 1], F32)
        one_c, B_one = sb("one_c", [128, 1], F32)
        tab_e, B_tab = sb("tab_e", [128, 8, 640], BF16)
        Kt, B_Kt = sb("Kt", [128, 4, 768], BF16)
        Vr, B_Vr = sb("Vr", [128, 6, 8 * 65], BF16)
        state, B_state = sb("state", [128, 512], F32)
        state_bf, B_statebf = sb("state_bf", [128, 512], BF16)

        banks = []
        for i in range(8):
            t = est.enter_context(nc.psum_tensor("pb%d" % i, [128, 512], F32))
            banks.append((t, Buf("pb%d" % i, excl=True)))
        class PSA:
            def __init__(self, idx):
                self.idx = list(idx)
                self.i = 0

            def next(self):
                b = banks[self.idx[self.i % len(self.idx)]]
                self.i += 1
                return b
        PSA_ALL = PSA(range(8))
        ps_state = {"cur": PSA_ALL}

        def ps():
            return ps_state["cur"].next()

        def drive(items):
            items = [list(it) for it in items]
            while items:
                for it in list(items):
                    gen, psa, w = it[0], it[1], it[2]
                    ps_state["cur"] = psa
                    try:
                        for _ in range(w):
                            next(gen)
                    except StopIteration:
                        items.remove(it)
                        if len(it) > 3:
                            items.extend([list(x) for x in it[3]()])
            ps_state["cur"] = PSA_ALL

        def chain(*gens):
            for g in gens:
                yield from g

        P.op("pool", lambda e: e.memset(ident_f[:], 0.0), writes=[B_idf])
        P.op("pool", lambda e: e.affine_select(out=ident_f[:], in_=ident_f[:], pattern=[[-1, 128]],
                                               compare_op=ALU.not_equal, fill=1.0, base=0, channel_multiplier=1),
             reads=[B_idf], writes=[B_idf])
        P.op("dve", lambda e: e.tensor_copy(out=ident_bf[:], in_=ident_f[:]), reads=[B_idf], writes=[B_idb])
        P.op("pool", lambda e: e.memset(ones_f[:], 1.0), writes=[B_ones])
        P.op("pool", lambda e: e.affine_select(out=tri_f[:], in_=ones_f[:], pattern=[[1, 128]],
                                               compare_op=ALU.is_ge, fill=0.0, base=0, channel_multiplier=-1),
             reads=[B_ones], writes=[B_tri])
        P.op("pool", lambda e: e.memset(maskneg[:], 0.0), writes=[B_mneg])
        P.op("pool", lambda e: e.affine_select(out=maskneg[:], in_=maskneg[:], pattern=[[1, 128]],
                                               compare_op=ALU.is_ge, fill=-30000.0, base=0, channel_multiplier=-1),
             reads=[B_mneg], writes=[B_mneg])
        P.op("pool", lambda e: e.memset(blk_bf[:], 0.0), writes=[B_blk])
        P.op("pool", lambda e: e.memset(blk_bf[0:64, 0:64], 1.0), writes=[B_blk])
        P.op("pool", lambda e: e.memset(blk_bf[64:128, 64:128], 1.0), writes=[B_blk])
        P.op("pool", lambda e: e.memset(eps_c[:], EPS), writes=[B_eps])
        P.op("pool", lambda e: e.memset(one_c[:], 1.0), writes=[B_one])
        P.op("pool", lambda e: e.memset(Vr[:], 1.0), writes=[B_Vr])
        P.op("pool", lambda e: e.memset(Kt[:], 0.0), writes=[B_Kt])
        B_blk_w = [Buf("wbs%d" % jb) for jb in range(NJB)]
        for jb in range(NJB):
            P.dma("pool", AP(wbs, jb * 128 * 3072, [[3072, 128], [128, 8], [1, 128]]), AP(w_gate, jb * 128, [[DFF, 128], [128 * DFF, 8], [1, 128]]), writes=[B_blk_w[jb]])
            P.dma("pool", AP(wbs, jb * 128 * 3072 + 1024, [[3072, 128], [128, 8], [1, 128]]), AP(w_up, jb * 128, [[DFF, 128], [128 * DFF, 8], [1, 128]]), writes=[B_blk_w[jb]])
            P.dma("pool", AP(wbs, jb * 128 * 3072 + 2048, [[3072, 128], [1, 1024]]), AP(w_down, jb * 128 * D, [[D, 128], [1, D]]), writes=[B_blk_w[jb]])

        P.dma("sp", s8bc[:], AP(small8, 0, [[0, 128], [8, 3], [1, 8]]), writes=[B_s8])
        P.dma("sp", ssdg_bc[:], AP(ssdg, 0, [[0, 128], [1, 512]]), writes=[B_ssdg])
        P.dma("sp", gq2[0:64, :], AP(qkg, 0, [[1, 64], [1, 1]]), writes=[B_gq])
        P.dma("sp", gq2[64:128, :], AP(qkg, 0, [[1, 64], [1, 1]]), writes=[B_gq])
        P.dma("sp", gk2[0:64, :], AP(qkg, 64, [[1, 64], [1, 1]]), writes=[B_gk])
        P.dma("sp", gk2[64:128, :], AP(qkg, 64, [[1, 64], [1, 1]]), writes=[B_gk])
        P.op("dve", lambda e: e.tensor_scalar(out=gq2[:], in0=gq2[:], scalar1=0.125, scalar2=None, op0=ALU.mult),
             reads=[B_gq], writes=[B_gq])
        P.op("act", lambda e: e.activation(out=s8bc[:, 1, :], in_=s8bc[:, 1, :], func=AF.Exp), reads=[B_s8], writes=[B_s8])
        P.op("dve", lambda e: e.tensor_scalar(out=s8bc[:, 1, :], in0=s8bc[:, 1, :], scalar1=-1.0, scalar2=None, op0=ALU.mult),
             reads=[B_s8], writes=[B_s8])
        dtb_bc = s8bc[:, 0, :]
        a_bc = s8bc[:, 1, :]
        dsk_bc = s8bc[:, 2, :]

        with contextlib.ExitStack() as tst:
            vrow = tst.enter_context(nc.sbuf_tensor("vrow", [7, D], F32))
            B_vrow = Buf("vrow")
            textsb = tst.enter_context(nc.sbuf_tensor("textsb", [8, 768], F32))
            B_textsb = Buf("textsb")

            P.dma("sp", vrow[:], AP(vecs, 0, [[D, 7], [1, D]]), writes=[B_vrow])
            bk, B_bk = ps()

            def tr_vec(e):
                ins = None
                for c in range(8):
                    ins = e.transpose(out=bk[:, c * 7:(c + 1) * 7], in_=vrow[0:7, c * 128:(c + 1) * 128], identity=ident_f[0:7, 0:7])
                return ins
            P.op("pe", tr_vec, reads=[B_vrow, B_idf], writes=[B_bk])
            P.op("act", lambda e: e.copy(out=vecT[:].rearrange("p c r -> p (c r)"), in_=bk[:, 0:56]), reads=[B_bk], writes=[B_vecT])

            NSTG = 4
            stg = [(tst.enter_context(nc.sbuf_tensor("stg%d" % i, [128, NPROJ], F32)), Buf("stg%d" % i)) for i in range(NSTG)]
            cast_i = [0]

            def cast(out_ap, in_ap, reads, writes):
                eng = ("act", "dve")[cast_i[0] % 2]
                cast_i[0] += 1
                if eng == "act":
                    P.op("act", lambda e: e.copy(out=out_ap, in_=in_ap), reads=reads, writes=writes)
                else:
                    P.op(eng, lambda e: e.tensor_copy(out=out_ap, in_=in_ap), reads=reads, writes=writes)
            si = 0
            for k in range(8):
                st_t, st_b = stg[si % NSTG]; si += 1
                P.dma("sp", st_t[:, 0:NPROJ], AP(w_in, k * 128 * NPROJ, [[NPROJ, 128], [1, NPROJ]]), writes=[st_b])
                cast(win[:, k, :], st_t[:, 0:NPROJ], [st_b], [B_win])
            for k in range(8):
                st_t, st_b = stg[si % NSTG]; si += 1
                P.dma("sp", st_t[:, 0:D], AP(w_out, k * 128 * D, [[D, 128], [1, D]]), writes=[st_b])
                cast(wout[:, k, :], st_t[:, 0:D], [st_b], [B_wout])
            P.dma("sp", textsb[:, 0:257], AP(relb, 0, [[257, 8], [1, 257]]), writes=[B_textsb])
            P.op("dve", lambda e: e.tensor_copy(out=textsb[:, 257:768], in_=textsb[:, 256:257].to_broadcast([8, 511])),
                 reads=[B_textsb], writes=[B_textsb])
            P.dma("sp", AP(text1, 0, [[768, 8], [1, 768]]), textsb[:], reads=[B_textsb], writes=[B_text1], store=True)
            text128 = tst.enter_context(nc.sbuf_tensor("text128", [128, 8, 768], F32))
            B_text128 = Buf("text128")
            P.dma("sp", text128[:], AP(text1, 0, [[0, 128], [768, 8], [1, 768]]), reads=[B_text1], writes=[B_text128])
            P.dma("sp", AP(text, 0, [[768, 128], [128 * 768, 8], [1, 768]]), text128[:], reads=[B_text128], writes=[B_text], store=True)
            tabf, B_tabf = text128[:, :, 0:640], B_text128
            P.dma("sp", tabf, AP(text, 128, [[767, 128], [128 * 768, 8], [1, 640]]), reads=[B_text], writes=[B_tabf])
            P.op("act", lambda e: e.activation(out=tab_e[:], in_=tabf, func=AF.Exp), reads=[B_tabf], writes=[B_tab])
            P.op("dve", lambda e: e.memset(tab_e[0:64, :, 576:640], 0.0), writes=[B_tab])
            P.op("dve", lambda e: e.memset(tab_e[64:128, :, 0:64], 0.0), writes=[B_tab])

            P.barrier(skip_prefix="L_wbs")

        x1g = [sb("x1g%d" % i, [128, 2, D], F32) for i in range(2)]
        xsc = Rot([sb("xsc%d" % i, [128, D], BF16) for i in range(2)])
        stats = Rot([sb("stat%d" % i, [128, 4], F32) for i in range(10)])
        hT, B_hT = sb("hT", [128, 8, 256], BF16)
        fT, B_fT = sb("fT", [128, 8, 256], BF16)
        _xp = sb("xpre", [128, 8, 3 + 256], BF16)
        xpre = [_xp, _xp]
        hist, B_hist = sb("hist", [128, 8, 3], BF16)
        xpost, B_xpost = sb("xpost", [128, 8, 256], BF16)
        cacc = Rot([sb("cacc%d" % i, [128, 256], F32) for i in range(3)])
        rawf = Rot([sb("rawf%d" % i, [128, 256], F32) for i in range(2)])
        sqb = Rot([sb("sqb%d" % i, [128, 256], BF16) for i in range(2)])
        sdf = Rot([sb("sdf%d" % i, [128, 256], F32) for i in range(2)])
        knf = Rot([sb("knf%d" % i, [128, 256], F32) for i in range(2)])
        qT, B_qT = sb("qT", [128, 8, 256], BF16)
        zs = [sb("zs%d" % i, [128, 512], BF16) for i in range(2)]
        dtt = [sb("dtt%d" % i, [128, 8], F32) for i in range(2)]
        sm8 = Rot([sb("sm8_%d" % i, [128, 8], F32) for i in range(8)])
        cdec, B_cdec = sb("cdec", [128, 8], F32)
        xsB = [sb("xsB%d" % i, [128, 768], BF16) for i in range(2)]
        rhs1, B_rhs1 = sb("rhs1", [128, 8, 128], F32)
        _r1 = rhs1[:].rearrange("p h q -> p (h q)")
        kst = [(_r1[:, 0:512], B_rhs1), (_r1[:, 512:1024], B_rhs1)]
        segf = Rot([sb("segf%d" % i, [128, 4, 128], F32) for i in range(1)])
        dec = Rot([sb("dec%d" % i, [128, 4, 128], BF16) for i in range(2)])
        Mt, B_Mt = sb("Mt", [128, 8, 128], BF16)
        ebc, B_ebc = sb("ebc", [128, 8, 128], BF16)
        Cp, B_Cp = sb("Cp", [128, 8, 128], BF16)
        xdt, B_xdt = sb("xdt", [128, 8, 64], BF16)
        xdte, B_xdte = sb("xdte", [128, 8, 64], BF16)
        ytmp = Rot([sb("ytmp%d" % i, [128, 512], F32) for i in range(1)])
        stmp, B_stmp = ytmp.items[0]
        _eb = [sb("ebuf%d" % i, [128, 4, 128], BF16) for i in range(2)]
        _pt = [sb("PT%d" % i, [128, 4, 128], BF16) for i in range(4)]
        _ebr, _ptr = Rot(_eb), Rot(_pt)
        ATT = [{"ob": (0, 1), "psa": [2, 3], "ebuf": _ebr, "PT": _ptr},
               {"ob": (6, 7), "psa": [0, 1], "ebuf": _ebr, "PT": _ptr}]
        rec = Rot([sb("rec%d" % i, [128, 4], F32) for i in range(2)])
        mix = [sb("mix%d" % i, [128, D], BF16) for i in range(2)]
        mixT = Rot([sb("mixT%d" % i, [128, 8, 128], BF16) for i in range(2)])
        sg = Rot([sb("sg%d" % i, [128, 256], F32) for i in range(2)])
        aT = Rot([sb("aT%d" % i, [128, 256], BF16) for i in range(3)])
        ost = Rot([sb("ost%d" % i, [128, 512], F32) for i in range(2)])

        P.op("dve", lambda e: e.memset(qT[:], 0.0), writes=[B_qT])
        gmixT = AP(vecT, 5, [[56, 128], [7, 8]])
        gffnT = AP(vecT, 6, [[56, 128], [7, 8]])

        def cw_ap(c, tap):
            return vecT[:, c, tap:tap + 1]

        out_toks = []

        def dump(name, ap, buf):
            if not DBG["dump"]:
                return
            dt = nc.dram_tensor("dbg_" + name, list(ap.shape), ap.dtype, kind="ExternalOutput")
            DBG["names"].append("dbg_" + name)
            out_toks.append(P.dma("sp", dt.ap(), ap, reads=[buf]))

        def act_sigmoid(dst_ap, src_ap, reads, dst_b):
            P.op("act", lambda e: e.activation(out=dst_ap, in_=src_ap, func=AF.Exp, scale=-1.0), reads=reads, writes=[dst_b])
            P.op("act", lambda e: e.activation(out=dst_ap, in_=dst_ap, func=AF.Ln, bias=one_c[0:dst_ap.shape[0], :]), reads=[dst_b, B_one], writes=[dst_b])
            P.op("act", lambda e: e.activation(out=dst_ap, in_=dst_ap, func=AF.Exp, scale=-1.0), reads=[dst_b], writes=[dst_b])

        def rstd_of(ss_ap, st_t, st_b, L, n):
            P.op("act", lambda e: e.activation(out=st_t[0:L, 1:2], in_=st_t[0:L, 0:1], func=AF.Ln, scale=1.0 / n, bias=eps_c[0:L, :]),
                 reads=[st_b, B_eps], writes=[st_b])
            P.op("act", lambda e: e.activation(out=st_t[0:L, 2:3], in_=st_t[0:L, 1:2], func=AF.Exp, scale=-0.5), reads=[st_b], writes=[st_b])

        def stage_norm_T(src, gT, dst_t, dst_b, L, t0=0, scratch=None):
            scaled = []
            for t, (xap, xb) in enumerate(src):
                st_t, st_b = stats.next()
                xs_t, xs_b = scratch[t] if scratch is not None else xsc.next()
                P.op("act", lambda e, xap=xap, st_t=st_t, xs_t=xs_t: e.activation(out=xs_t[0:L, :], in_=xap, func=AF.Square, accum_out=st_t[0:L, 0:1]),
                     reads=[xb], writes=[xs_b, st_b])
                scaled.append((xs_t, xs_b, st_t, st_b, xap, xb))
            yield
            for (xs_t, xs_b, st_t, st_b, xap, xb) in scaled:
                rstd_of(None, st_t, st_b, L, D)
            yield
            for (xs_t, xs_b, st_t, st_b, xap, xb) in scaled:
                P.op("dve", lambda e, xap=xap, st_t=st_t, xs_t=xs_t: e.tensor_scalar(out=xs_t[0:L, :], in0=xap, scalar1=st_t[0:L, 2:3], scalar2=None, op0=ALU.mult),
                     reads=[xb, st_b], writes=[xs_b])
            yield
            for t, (xs_t, xs_b, st_t, st_b, xap, xb) in enumerate(scaled, start=t0):
                bk, bkb = ps()
                bkv = bk[:].bitcast(BF16)

                def tr(e, xs_t=xs_t, bkv=bkv):
                    ins = None
                    for k in range(8):
                        ins = e.transpose(out=bkv[:, k * L:(k + 1) * L], in_=xs_t[0:L, k * 128:(k + 1) * 128], identity=ident_bf[0:L, 0:L])
                    return ins
                P.op("pe", tr, reads=[xs_b, B_idb], writes=[bkb])
                P.op("dve", lambda e, bkv=bkv, t=t: e.tensor_tensor(out=dst_t[:, :, t * L:(t + 1) * L],
                                                                   in0=bkv[:, 0:8 * L].rearrange("p (k l) -> p k l", k=8),
                                                                   in1=bc3(gT, L), op=ALU.mult),
                     reads=[bkb, B_vecT], writes=[dst_b])
                yield

        def fm_mm(col0, T):
            bk, bkb = ps()

            def mm(e, bk=bk):
                ins = None
                for k in range(8):
                    ins = e.matmul(bk[:, 0:T], lhsT=win[:, k, col0:col0 + 128], rhs=hT[:, k, 0:T], start=(k == 0), stop=(k == 7))
                return ins
            P.op("pe", mm, reads=[B_win, B_hT], writes=[bkb])
            return bk, bkb

        def stage_inproj(G):
            T, L, nt = G["T"], G["L"], G["nt"]
            xp_t, xp_b = xpre[G["par"]]
            for c in range(8):
                bk, bkb = fm_mm(C_XBC + 128 * c, T)
                P.op("act", lambda e, bk=bk, c=c: e.copy(out=xp_t[:, c, 3:3 + T], in_=bk[:, 0:T]), reads=[bkb], writes=[xp_b])
                yield

        def stage_inproj_qk(G):
            T, L, nt = G["T"], G["L"], G["nt"]
            def qk_store(which, hp, rw_t, rw_b, sd_t, sd_b):
                if which == "q":
                    P.op("dve", lambda e: e.scalar_tensor_tensor(
                        out=qT[0:64, 2 * hp, 0:T], in0=rw_t[0:64, 0:T], scalar=gq2[0:64, 0:1], in1=sd_t[0:64, 0:T], op0=ALU.mult, op1=ALU.mult),
                        reads=[rw_b, sd_b, B_gq], writes=[B_qT])
                    P.op("dve", lambda e: e.scalar_tensor_tensor(
                        out=qT[64:128, 2 * hp + 1, 0:T], in0=rw_t[64:128, 0:T], scalar=gq2[64:128, 0:1], in1=sd_t[64:128, 0:T], op0=ALU.mult, op1=ALU.mult),
                        reads=[rw_b, sd_b, B_gq], writes=[B_qT])
                else:
                    c0 = G["slot0"] * 128
                    P.op("dve", lambda e: e.scalar_tensor_tensor(
                        out=Kt[:, hp, c0:c0 + T], in0=rw_t[:, 0:T], scalar=gk2[:, 0:1], in1=sd_t[:, 0:T], op0=ALU.mult, op1=ALU.mult),
                        reads=[rw_b, sd_b, B_gk], writes=[B_Kt])
                    if G["kv_out"] is not None and DBG["kv"] in ("both", "k"):
                        kn_t, kn_b = knf.next()
                        P.op("dve", lambda e: e.scalar_tensor_tensor(
                            out=kn_t[:, 0:T], in0=rw_t[:, 0:T], scalar=gk2[:, 0:1], in1=sd_t[:, 0:T], op0=ALU.mult, op1=ALU.mult),
                            reads=[rw_b, sd_b, B_gk], writes=[kn_b])
                        for t in range(nt):
                            bkk, bkkb = ps()
                            P.op("pe", lambda e, bkk=bkk, t=t: e.transpose(out=bkk[0:L, 0:128], in_=kn_t[:, t * L:(t + 1) * L], identity=ident_f[:]),
                                 reads=[kn_b, B_idf], writes=[bkkb])
                            ks_t, ks_b = kst[t]
                            P.op("act", lambda e, bkk=bkk, ks_t=ks_t: e.copy(out=ks_t[0:L, hp * 128:(hp + 1) * 128], in_=bkk[0:L, 0:128]),
                                 reads=[bkkb], writes=[ks_b])

            chunks = [(which, hp) for which in ("q", "k") for hp in range(4)]
            p_cs, p_blk, p_ln, p_st = [], [], [], []
            for c in range(len(chunks) + 2):
                if p_st:
                    qk_store(*p_st.pop(0))
                if c < len(chunks):
                    which, hp = chunks[c]
                    col0 = (C_Q if which == "q" else C_K) + 128 * hp
                    bk, bkb = fm_mm(col0, T)
                    p_cs.append((which, hp, bk, bkb))
                if p_blk:
                    (which1, hp1, rw_t, rw_b, sq_t, sq_b) = p_blk.pop(0)
                    bk2, bk2b = ps()
                    P.op("pe", lambda e, bk2=bk2, sq_t=sq_t: e.matmul(bk2[:, 0:T], lhsT=blk_bf[:], rhs=sq_t[:, 0:T], start=True, stop=True),
                         reads=[sq_b, B_blk], writes=[bk2b])
                    p_ln.append((which1, hp1, rw_t, rw_b, bk2, bk2b))
                yield
                if p_cs:
                    (which0, hp0, bk0, bk0b) = p_cs.pop(0)
                    rw_t, rw_b = rawf.next()
                    P.op("act", lambda e, bk0=bk0, rw_t=rw_t: e.copy(out=rw_t[:, 0:T], in_=bk0[:, 0:T]), reads=[bk0b], writes=[rw_b])
                    sq_t, sq_b = sqb.next()
                    P.op("act", lambda e, bk0=bk0, sq_t=sq_t: e.activation(out=sq_t[:, 0:T], in_=bk0[:, 0:T], func=AF.Square),
                         reads=[bk0b], writes=[sq_b])
                    p_blk.append((which0, hp0, rw_t, rw_b, sq_t, sq_b))
                if p_ln:
                    (which1, hp1, rw1_t, rw1_b, bk2, bk2b) = p_ln.pop(0)
                    sd_t, sd_b = sdf.next()
                    P.op("act", lambda e, bk2=bk2, sd_t=sd_t: e.activation(out=sd_t[:, 0:T], in_=bk2[:, 0:T], func=AF.Ln, scale=1.0 / 64, bias=eps_c[:]),
                         reads=[bk2b, B_eps], writes=[sd_b])
                    P.op("act", lambda e, sd_t=sd_t: e.activation(out=sd_t[:, 0:T], in_=sd_t[:, 0:T], func=AF.Exp, scale=-0.5), reads=[sd_b], writes=[sd_b])
                    p_st.append((which1, hp1, rw1_t, rw1_b, sd_t, sd_b))
                yield
            while p_st:
                qk_store(*p_st.pop(0))
            yield
            if G["kv_out"] is not None and DBG["kv"] in ("both", "k"):
                for t in range(nt):
                    ks_t, ks_b = kst[t]
                    out_toks.append(P.dma("sp", G["kv_out"][0](t), ks_t[0:L, :], reads=[ks_b]))

        def stage_inproj_tok(G):
            T, L, nt = G["T"], G["L"], G["nt"]
            for t in range(nt):
                cols = slice(t * L, (t + 1) * L)
                bk, bkb = ps()

                def mmz(e, bk=bk, cols=cols):
                    ins = None
                    for k in range(8):
                        ins = e.matmul(bk[0:L, :], lhsT=hT[:, k, cols], rhs=win[:, k, C_Z:C_Z + 512], start=(k == 0), stop=(k == 7))
                    return ins
                P.op("pe", mmz, reads=[B_win, B_hT], writes=[bkb])
                z_t, z_b = zs[t]
                zt_t, zt_b = ytmp.items[0]
                bkz, bkzb = bk, bkb
                yield
                act_sigmoid(zt_t[0:L, :], bkz[0:L, :], [bkzb], zt_b)
                bk, bkb = ps()

                def mmv(e, bk=bk, cols=cols):
                    ins = None
                    for k in range(8):
                        ins = e.matmul(bk[0:L, :], lhsT=hT[:, k, cols], rhs=win[:, k, C_V:C_V + 512], start=(k == 0), stop=(k == 7))
                    return ins
                P.op("pe", mmv, reads=[B_win, B_hT], writes=[bkb])
                yield
                P.op("dve", lambda e, bkz=bkz, z_t=z_t, zt_t=zt_t: e.tensor_tensor(out=z_t[0:L, :], in0=zt_t[0:L, :], in1=bkz[0:L, :], op=ALU.mult),
                     reads=[bkzb, zt_b], writes=[z_b])
                slot = G["slot0"] + t
                vdst = AP(Vr, slot * 520, [[6 * 520, L], [65, 8], [1, 64]])
                P.op("act", lambda e, bk=bk, vdst=vdst: e.copy(out=vdst, in_=bk[0:L, :].rearrange("p (h d) -> p h d", h=8)),
                     reads=[bkb], writes=[B_Vr])
                if G["kv_out"] is not None and DBG["kv"] in ("both", "v"):
                    o_t, o_b = ost.next()
                    P.op("dve", lambda e, bk=bk, o_t=o_t: e.tensor_copy(out=o_t[0:L, :], in_=bk[0:L, :]), reads=[bkb], writes=[o_b])
                    out_toks.append(P.dma("sp", G["kv_out"][1](t), o_t[0:L, :], reads=[o_b]))
                yield
                bk, bkb = ps()

                def mmd(e, bk=bk, cols=cols):
                    ins = None
                    for k in range(8):
                        ins = e.matmul(bk[0:L, 0:8], lhsT=hT[:, k, cols], rhs=win[:, k, C_DT:C_DT + 8], start=(k == 0), stop=(k == 7))
                    return ins
                P.op("pe", mmd, reads=[B_win, B_hT], writes=[bkb])
                d_t, d_b = dtt[t]
                yield
                P.op("dve", lambda e, bk=bk, d_t=d_t: e.tensor_tensor(out=d_t[0:L, :], in0=bk[0:L, 0:8], in1=dtb_bc[0:L, :], op=ALU.add),
                     reads=[bkb, B_s8], writes=[d_b])
                yield
                P.op("act", lambda e, d_t=d_t: e.activation(out=d_t[0:L, :], in_=d_t[0:L, :], func=AF.Exp), reads=[d_b], writes=[d_b])
                P.op("act", lambda e, d_t=d_t: e.activation(out=d_t[0:L, :], in_=d_t[0:L, :], func=AF.Ln, bias=one_c[0:L, :]), reads=[d_b, B_one], writes=[d_b])
                yield

        def stage_conv(G):
            T = G["T"]
            xp_t, xp_b = xpre[G["par"]]
            s1, s2 = [], []
            for c in range(8 + 2):
                if s2:
                    (c2, ca_t, ca_b, sgm_t, sgm_b) = s2.pop(0)
                    P.op("dve", lambda e, c2=c2, ca_t=ca_t, sgm_t=sgm_t: e.tensor_tensor(out=xpost[:, c2, 0:T], in0=ca_t[:, 0:T], in1=sgm_t[:, 0:T], op=ALU.mult),
                         reads=[ca_b, sgm_b], writes=[B_xpost])
                if s1:
                    (c1, ca_t, ca_b) = s1.pop(0)
                    sgm_t, sgm_b = sdf.next()
                    act_sigmoid(sgm_t[:, 0:T], ca_t[:, 0:T], [ca_b], sgm_b)
                    s2.append((c1, ca_t, ca_b, sgm_t, sgm_b))
                if c < 8:
                    ca_t, ca_b = cacc.next()
                    P.op("dve", lambda e, c=c, ca_t=ca_t: e.tensor_scalar(out=ca_t[:, 0:T], in0=xp_t[:, c, 0:T], scalar1=cw_ap(c, 0), scalar2=cw_ap(c, 4),
                                                                          op0=ALU.mult, op1=ALU.add),
                         reads=[xp_b, B_vecT], writes=[ca_b])
                    for tap in range(1, 4):
                        P.op("dve", lambda e, c=c, ca_t=ca_t, tap=tap: e.scalar_tensor_tensor(
                            out=ca_t[:, 0:T], in0=xp_t[:, c, tap:tap + T], scalar=cw_ap(c, tap), in1=ca_t[:, 0:T], op0=ALU.mult, op1=ALU.add),
                            reads=[xp_b, B_vecT, ca_b], writes=[ca_b])
                    s1.append((c, ca_t, ca_b))
                yield
            P.op("dve", lambda e: e.tensor_copy(out=hist[:], in_=xp_t[:, :, T:T + 3]), reads=[xp_b], writes=[B_hist])

        def stage_xsB(G):
            L, nt = G["L"], G["nt"]
            for t in range(nt):
                bk, bkb = ps()
                bkv = bk[:].bitcast(BF16)

                def tr(e, bkv=bkv, t=t):
                    ins = None
                    for c in range(6):
                        ins = e.transpose(out=bkv[0:L, c * 128:(c + 1) * 128], in_=xpost[:, c, t * L:(t + 1) * L], identity=ident_bf[:])
                    return ins
                P.op("pe", tr, reads=[B_xpost, B_idb], writes=[bkb])
                x_t, x_b = xsB[t]
                P.op("act", lambda e, bkv=bkv, x_t=x_t: e.copy(out=x_t[0:L, :], in_=bkv[0:L, 0:768]), reads=[bkb], writes=[x_b])
                yield

        def stage_ssd(G, t):
            L = G["L"]
            has_state = G["seq"]["has_state"]
            cols = slice(t * L, (t + 1) * L)
            d_t, d_b = dtt[t]
            x_t, x_b = xsB[t]
            z_t, z_b = zs[t]
            da_t, da_b = sm8.next()
            P.op("dve", lambda e: e.tensor_tensor(out=da_t[0:L, :], in0=d_t[0:L, :], in1=a_bc[0:L, :], op=ALU.mult), reads=[d_b, B_s8], writes=[da_b])
            P.op("dve", lambda e: e.tensor_tensor(out=rhs1[0:L, :, 0:L], in0=AP(tri_f, 0, [[128, L], [0, 8], [1, L]]),
                                                   in1=bc3(da_t[0:L, :], L), op=ALU.mult),
                 reads=[B_tri, da_b], writes=[B_rhs1])
            yield
            bkc, bkcb = ps()
            P.op("pe", lambda e: e.matmul(bkc[0:L, 0:8], lhsT=tri_f[0:L, 0:L], rhs=da_t[0:L, :], start=True, stop=True),
                 reads=[B_tri, da_b], writes=[bkcb])
            dct_t, dct_b = sm8.next()
            P.op("act", lambda e: e.copy(out=dct_t[0:L, :], in_=bkc[0:L, 0:8]), reads=[bkcb], writes=[dct_b])
            P.op("dve", lambda e: e.tensor_tensor(out=xdt[0:L, :, :], in0=x_t[0:L, 0:512].rearrange("p (h d) -> p h d", h=8),
                                                   in1=bc3(d_t[0:L, :], 64), op=ALU.mult),
                 reads=[x_b, d_b], writes=[B_xdt])
            sdte_t, sdte_b = sm8.next()
            yield
            for hq in range(2):
                bkb_t, bkb_b = ps()
                P.op("pe", lambda e, bkb_t=bkb_t, hq=hq: e.matmul(bkb_t[:, 0:4 * L].rearrange("p (h q) -> p h q", h=4), lhsT=ones_f[0:L, :],
                                                                  rhs=rhs1[0:L, 4 * hq:4 * hq + 4, 0:L], start=True, stop=True),
                     reads=[B_ones, B_rhs1], writes=[bkb_b])
                bcv = bkb_t[:, 0:4 * L].rearrange("p (h q) -> p h q", h=4)
                bkd, bkdb = ps()
                P.op("pe", lambda e, bkd=bkd, hq=hq: e.matmul(bkd[0:L, 0:L], lhsT=xpost[:, 4 + hq, cols], rhs=xpost[:, 6 + hq, cols], start=True, stop=True),
                     reads=[B_xpost], writes=[bkdb])
                yield
                sg_t, sg_b = segf.next()
                P.op("dve", lambda e, bcv=bcv, sg_t=sg_t, hq=hq: e.tensor_tensor(out=sg_t[0:L, :, 0:L], in0=bcv[0:L], in1=bc3(dct_t[0:L, 4 * hq:4 * hq + 4], L), op=ALU.subtract),
                     reads=[bkb_b, dct_b], writes=[sg_b])
                P.op("dve", lambda e, bcv=bcv, hq=hq: e.tensor_tensor(out=sdte_t[0:L, 4 * hq:4 * hq + 4], in0=bcv[0:L, :, L - 1],
                                                                      in1=dct_t[0:L, 4 * hq:4 * hq + 4], op=ALU.subtract),
                     reads=[bkb_b, dct_b], writes=[sdte_b])
                P.op("dve", lambda e, sg_t=sg_t: e.tensor_tensor(out=sg_t[0:L, :, 0:L], in0=sg_t[0:L, :, 0:L],
                                                                  in1=AP(maskneg, 0, [[128, L], [0, 4], [1, L]]), op=ALU.add),
                     reads=[sg_b, B_mneg], writes=[sg_b])
                yield
                P.op("act", lambda e, bcv=bcv, hq=hq: e.activation(out=ebc[:, 4 * hq:4 * hq + 4, 0:L], in_=bcv, func=AF.Exp), reads=[bkb_b], writes=[B_ebc])
                P.op("act", lambda e, bcv=bcv, hq=hq: e.activation(out=cdec[:, 4 * hq:4 * hq + 4], in_=bcv[:, :, L - 1], func=AF.Exp), reads=[bkb_b], writes=[B_cdec])
                dc_t, dc_b = dec.next()
                P.op("act", lambda e, sg_t=sg_t, dc_t=dc_t: e.activation(out=dc_t[0:L, :, 0:L], in_=sg_t[0:L, :, 0:L], func=AF.Exp), reads=[sg_b], writes=[dc_b])
                yield
                P.op("dve", lambda e, dc_t=dc_t, hq=hq, bkd=bkd: e.tensor_tensor(out=Mt[0:L, 4 * hq:4 * hq + 4, 0:L], in0=dc_t[0:L, :, 0:L],
                                                                                 in1=AP(bkd, 0, [[512, L], [0, 4], [1, L]]), op=ALU.mult),
                     reads=[dc_b, bkdb], writes=[B_Mt])
                if has_state:
                    P.op("dve", lambda e, hq=hq: e.tensor_tensor(out=Cp[:, 4 * hq:4 * hq + 4, 0:L], in0=ebc[:, 4 * hq:4 * hq + 4, 0:L],
                                                                  in1=AP(xpost, (6 + hq) * 256 + t * L, [[2048, 128], [0, 4], [1, L]]), op=ALU.mult),
                         reads=[B_ebc, B_xpost], writes=[B_Cp])
                yield
            P.op("act", lambda e: e.activation(out=sdte_t[0:L, :], in_=sdte_t[0:L, :], func=AF.Exp), reads=[sdte_b], writes=[sdte_b])
            P.op("dve", lambda e: e.tensor_tensor(out=xdte[0:L, :, :], in0=xdt[0:L, :, :], in1=bc3(sdte_t[0:L, :], 64), op=ALU.mult),
                 reads=[B_xdt, sdte_b], writes=[B_xdte])
            bky, bkyb = ps()

            def mmy(e):
                ins = None
                for h in range(8):
                    ins = e.matmul(bky[0:L, h * 64:(h + 1) * 64], lhsT=Mt[0:L, h, 0:L], rhs=xdt[0:L, h, :], start=True, stop=(not has_state))
                    if has_state:
                        ins = e.matmul(bky[0:L, h * 64:(h + 1) * 64], lhsT=Cp[:, h, 0:L], rhs=state_bf[:, h * 64:(h + 1) * 64], start=False, stop=True)
                return ins
            P.op("pe", mmy, reads=[B_Mt, B_xdt] + ([B_Cp, B_statebf] if has_state else []), writes=[bkyb])
            yield
            yield
            bks, bksb = ps()

            def mmst(e):
                ins = None
                for h in range(8):
                    g = h // 4
                    ins = e.matmul(bks[:, h * 64:(h + 1) * 64], lhsT=x_t[0:L, 512 + 128 * g:512 + 128 * (g + 1)], rhs=xdte[0:L, h, :], start=True, stop=True)
                return ins
            P.op("pe", mmst, reads=[x_b, B_xdte], writes=[bksb])
            if has_state:
                P.op("dve", lambda e: e.tensor_tensor(out=stmp[:].rearrange("p (h d) -> p h d", h=8), in0=state[:].rearrange("p (h d) -> p h d", h=8),
                                                      in1=bc3(cdec[:, :], 64), op=ALU.mult),
                     reads=[B_state, B_cdec], writes=[B_stmp])
                P.op("dve", lambda e: e.tensor_tensor(out=state[:], in0=stmp[:], in1=bks[:], op=ALU.add), reads=[B_stmp, bksb], writes=[B_state])
            else:
                P.op("dve", lambda e: e.tensor_copy(out=state[:], in_=bks[:]), reads=[bksb], writes=[B_state])
            P.op("act", lambda e: e.copy(out=state_bf[:], in_=state[:]), reads=[B_state], writes=[B_statebf])
            G["seq"]["has_state"] = True
            yield
            y_t, y_b = ytmp.next()
            P.op("dve", lambda e: e.tensor_tensor(out=y_t[0:L, :].rearrange("p (h d) -> p h d", h=8), in0=x_t[0:L, 0:512].rearrange("p (h d) -> p h d", h=8),
                                                   in1=bc3(dsk_bc[0:L, :], 64), op=ALU.mult),
                 reads=[x_b, B_s8], writes=[y_b])
            P.op("dve", lambda e: e.tensor_tensor(out=y_t[0:L, :], in0=y_t[0:L, :], in1=bky[0:L, :], op=ALU.add), reads=[y_b, bkyb], writes=[y_b])
            P.op("dve", lambda e: e.tensor_tensor(out=y_t[0:L, :], in0=y_t[0:L, :], in1=z_t[0:L, :], op=ALU.mult), reads=[y_b, z_b], writes=[y_b])
            yield
            st_t, st_b = stats.next()
            m_t, m_b = mix[t]
            P.op("act", lambda e: e.activation(out=m_t[0:L, 0:512], in_=y_t[0:L, :], func=AF.Square, accum_out=st_t[0:L, 0:1]),
                 reads=[y_b], writes=[m_b, st_b])
            rstd_of(None, st_t, st_b, L, 512)
            P.op("dve", lambda e: e.scalar_tensor_tensor(out=m_t[0:L, 0:512], in0=y_t[0:L, :], scalar=st_t[0:L, 2:3], in1=ssdg_bc[0:L, :],
                                                         op0=ALU.mult, op1=ALU.mult),
                 reads=[y_b, st_b, B_ssdg], writes=[m_b])

        def stage_attn(G, t):
            L = G["L"]
            ti = G["ti0"] + t
            m_t, m_b = mix[t]
            keys = []
            for j in range(-4, 1):
                kt = ti + j
                if kt < 0:
                    continue
                Lk = L if j == 0 else 128
                keys.append((j, kt % 6, Lk))
            res = ATT[t % 2]
            ob = [banks[res["ob"][0]], banks[res["ob"][1]]]
            ebuf, PT = res["ebuf"], res["PT"]
            pend = []

            def emit_pv(item):
                (j, slot, Lk, hq, pt_t, pt_b, first, last) = item
                o_t, o_b = ob[hq]

                def pv(e):
                    ins = None
                    for hh in range(4):
                        h = 4 * hq + hh
                        ins = e.matmul(o_t[0:L, hh * 65:(hh + 1) * 65], lhsT=pt_t[0:Lk, hh, 0:L],
                                       rhs=AP(Vr, slot * 520 + h * 65, [[6 * 520, Lk], [1, 65]]), start=(first and hh == 0), stop=last)
                    return ins
                P.op("pe", pv, reads=[pt_b, B_Vr], writes=[o_b])

            for idx, (j, slot, Lk) in enumerate(keys):
                for hq in range(2):
                    bk, bkb = ps()

                    def qk(e, bk=bk, hq=hq, slot=slot, Lk=Lk):
                        ins = None
                        for hh in range(4):
                            h = 4 * hq + hh
                            hp = h // 2
                            ins = e.matmul(bk[0:Lk, hh * L:(hh + 1) * L], lhsT=Kt[:, hp, slot * 128:slot * 128 + Lk],
                                           rhs=qT[:, h, t * L:(t + 1) * L], start=True, stop=True)
                        return ins
                    P.op("pe", qk, reads=[B_Kt, B_qT], writes=[bkb])
                    e_t, e_b = ebuf.next()
                    P.op("act", lambda e, bk=bk, e_t=e_t, Lk=Lk: e.activation(out=e_t[0:Lk, :, 0:L], in_=bk[0:Lk, 0:4 * L].rearrange("p (h q) -> p h q", h=4), func=AF.Exp),
                         reads=[bkb], writes=[e_b])
                    pt_t, pt_b = PT.next()
                    c0 = -128 * j
                    P.op("dve", lambda e, e_t=e_t, pt_t=pt_t, Lk=Lk, hq=hq, c0=c0: e.tensor_tensor(
                        out=pt_t[0:Lk, :, 0:L], in0=e_t[0:Lk, :, 0:L], in1=tab_e[0:Lk, 4 * hq:4 * hq + 4, c0:c0 + L], op=ALU.mult),
                        reads=[e_b, B_tab], writes=[pt_b])
                    pend.append((j, slot, Lk, hq, pt_t, pt_b, idx == 0, idx == len(keys) - 1))
                    if len(pend) > 2:
                        emit_pv(pend.pop(0))
                    yield
            while pend:
                emit_pv(pend.pop(0))
            for hq in range(2):
                o_t, o_b = ob[hq]
                r_t, r_b = rec.next()
                P.op("dve", lambda e, o_t=o_t, r_t=r_t: e.reciprocal(out=r_t[0:L, :], in_=AP(o_t, 64, [[512, L], [65, 4]])), reads=[o_b], writes=[r_b])
                P.op("dve", lambda e, o_t=o_t, r_t=r_t, hq=hq: e.tensor_tensor(
                    out=m_t[0:L, 512 + 256 * hq:512 + 256 * (hq + 1)].rearrange("p (h d) -> p h d", h=4),
                    in0=AP(o_t, 0, [[512, L], [65, 4], [1, 64]]), in1=bc3(r_t[0:L, :], 64), op=ALU.mult),
                    reads=[o_b, r_b], writes=[m_b])
            yield

        def stage_outproj(G, tiles):
            L, nt = G["L"], G["nt"]
            x_t, x_b = x1g[G["par"]]
            mts = {}
            for t in tiles:
                m_t, m_b = mix[t]
                bk, bkb = ps()
                bkv = bk[:].bitcast(BF16)

                def tr(e, m_t=m_t, bkv=bkv):
                    ins = None
                    for k in range(8):
                        ins = e.transpose(out=bkv[:, k * L:(k + 1) * L], in_=m_t[0:L, k * 128:(k + 1) * 128], identity=ident_bf[0:L, 0:L])
                    return ins
                P.op("pe", tr, reads=[m_b, B_idb], writes=[bkb])
                mt_t, mt_b = mixT.next()
                P.op("act", lambda e, mt_t=mt_t, bkv=bkv: e.copy(out=mt_t[:, :, 0:L], in_=bkv[:, 0:8 * L].rearrange("p (k l) -> p k l", k=8)), reads=[bkb], writes=[mt_b])
                mts[t] = (mt_t, mt_b)
                yield
            for t in tiles:
                mt_t, mt_b = mts[t]
                for half in range(2):
                    bk2, bk2b = ps()

                    def mm(e, bk2=bk2, half=half, mt_t=mt_t):
                        ins = None
                        for k in range(8):
                            ins = e.matmul(bk2[0:L, :], lhsT=mt_t[:, k, 0:L], rhs=wout[:, k, half * 512:(half + 1) * 512], start=(k == 0), stop=(k == 7))
                        return ins
                    P.op("pe", mm, reads=[mt_b, B_wout], writes=[bk2b])
                    P.op("dve", lambda e, bk2=bk2, half=half, t=t: e.tensor_tensor(out=x_t[0:L, t, half * 512:(half + 1) * 512],
                                                                              in0=x_t[0:L, t, half * 512:(half + 1) * 512], in1=bk2[0:L, :], op=ALU.add),
                         reads=[bk2b, x_b], writes=[x_b])
                    yield

        def load_ffn_gu(jb):
            w_t, w_b = wgu[jb % 3]
            P.dma("sp", w_t[:], AP(wbs, jb * 128 * 3072, [[3072, 128], [1, 2048]]), reads=[B_blk_w[jb]], writes=[w_b])

        def load_ffn_dn(jb):
            d_t, d_b = wdn[jb % 3]
            P.dma("sp", d_t[:], AP(wbs, jb * 128 * 3072 + 2048, [[3072, 128], [1, 1024]]), reads=[B_blk_w[jb]], writes=[d_b])

        def stage_ffn(G):
            T, L, nt = G["T"], G["L"], G["nt"]
            x_t, x_b = x1g[G["par"]]
            acc = {}
            for t in range(nt):
                for half in range(2):
                    acc[(t, half)] = banks[t * 2 + half]
            pend = []

            def emit_down(item):
                (j, s, a_t, a_b) = item
                d_t, d_b = wdn[s]

                def mm(e):
                    ins = None
                    for t in range(nt):
                        for half in range(2):
                            ins = e.matmul(acc[(t, half)][0][0:L, :], lhsT=a_t[:, t * L:(t + 1) * L], rhs=d_t[:, half * 512:(half + 1) * 512],
                                           start=(j == 0), stop=(j == NJB - 1))
                    return ins
                P.op("pe", mm, reads=[a_b, d_b], writes=[acc[k][1] for k in acc])
                if j + 3 < NJB:
                    load_ffn_dn(j + 3)

            p_sig, p_mul = [], []
            for jb in range(NJB + 2):
                if p_sig:
                    (j1, s1, bk1, bk1b) = p_sig.pop(0)
                    s_t, s_b = sg.next()
                    act_sigmoid(s_t[:, 0:T], bk1[:, 0:T], [bk1b], s_b)
                    p_mul.append((j1, s1, bk1, bk1b, s_t, s_b))
                if jb < NJB:
                    s = jb % 3
                    w_t, w_b = wgu[s]
                    g_t = w_t[:, 0:1024].rearrange("p (k n) -> p k n", k=8)
                    u_t = w_t[:, 1024:2048].rearrange("p (k n) -> p k n", k=8)
                    bk, bkb = ps()

                    def mm(e, bk=bk, g_t=g_t, u_t=u_t):
                        ins = None
                        for k in range(8):
                            ins = e.matmul(bk[:, 0:T], lhsT=g_t[:, k, :], rhs=fT[:, k, 0:T], start=(k == 0), stop=(k == 7))
                        for k in range(8):
                            ins = e.matmul(bk[:, 256:256 + T], lhsT=u_t[:, k, :], rhs=fT[:, k, 0:T], start=(k == 0), stop=(k == 7))
                        return ins
                    P.op("pe", mm, reads=[w_b, B_fT], writes=[bkb])
                    if jb + 3 < NJB:
                        load_ffn_gu(jb + 3)
                    p_sig.append((jb, s, bk, bkb))
                yield
                if pend:
                    emit_down(pend.pop(0))
                yield
                if p_mul:
                    (j1, s1, bk1, bk1b, s_t, s_b) = p_mul.pop(0)
                    P.op("dve", lambda e, bk1=bk1, s_t=s_t: e.tensor_tensor(out=s_t[:, 0:T], in0=s_t[:, 0:T], in1=bk1[:, 0:T], op=ALU.mult),
                         reads=[s_b, bk1b], writes=[s_b])
                    a_t, a_b = aT.next()
                    P.op("dve", lambda e, bk1=bk1, s_t=s_t, a_t=a_t: e.tensor_tensor(out=a_t[:, 0:T], in0=s_t[:, 0:T], in1=bk1[:, 256:256 + T], op=ALU.mult),
                         reads=[s_b, bk1b], writes=[a_b])
                    pend.append((j1, s1, a_t, a_b))
                yield
            while pend:
                emit_down(pend.pop(0))
            for t in range(nt):
                for half in range(2):
                    a_t, a_b = acc[(t, half)]
                    P.op("dve", lambda e, a_t=a_t, t=t, half=half: e.tensor_tensor(out=x_t[0:L, t, half * 512:(half + 1) * 512],
                                                                                   in0=x_t[0:L, t, half * 512:(half + 1) * 512], in1=a_t[0:L, :], op=ALU.add),
                         reads=[a_b, x_b], writes=[x_b])
                out_toks.append(P.dma("sp", G["y_out"](t), x_t[0:L, t, :], reads=[x_b]))
                yield

        def emit_state_out(dst_tensor, dst_off):
            bk, bkb = ps()

            def tr(e):
                ins = None
                for hp in range(4):
                    ins = e.transpose(out=bk[:, hp * 128:(hp + 1) * 128], in_=state[:, hp * 128:(hp + 1) * 128], identity=ident_f[:])
                return ins
            P.op("pe", tr, reads=[B_state, B_idf], writes=[bkb])
            o_t, o_b = ost.next()
            P.op("act", lambda e: e.copy(out=o_t[:], in_=bk[:]), reads=[bkb], writes=[o_b])
            out_toks.append(P.dma("sp", AP(dst_tensor, dst_off, [[128, 128], [128 * 128, 4], [1, 128]]),
                                  o_t[:].rearrange("p (a n) -> p a n", a=4), reads=[o_b]))

        def emit_conv_out(G, dst_tensor, dst_off):
            T = G["T"]
            xp_t, xp_b = xpre[G["par"]]
            bk, bkb = ps()
            bkv = bk[:].bitcast(BF16)

            def tr(e):
                ins = None
                for c in range(8):
                    ins = e.transpose(out=bkv[0:3, c * 128:(c + 1) * 128], in_=xp_t[:, c, T:T + 3], identity=ident_bf[:])
                return ins
            P.op("pe", tr, reads=[xp_b, B_idb], writes=[bkb])
            for half in range(2):
                o_t, o_b = ost.next()
                P.op("act", lambda e, o_t=o_t, half=half: e.copy(out=o_t[0:3, :], in_=bkv[0:3, half * 512:(half + 1) * 512]), reads=[bkb], writes=[o_b])
                out_toks.append(P.dma("sp", AP(dst_tensor, dst_off + half * 512, [[D, 3], [1, 512]]), o_t[0:3, :], reads=[o_b]))

        def gen_head(G, prev):
            T, L, nt = G["T"], G["L"], G["nt"]
            x_t, x_b = x1g[G["par"]]
            xp_t, xp_b = xpre[G["par"]]
            if G["seq"]["kind"] == "sample":
                yield from gen_sample_init(G)
            elif prev is not None:
                P.op("dve", lambda e: e.tensor_copy(out=xp_t[:, :, 0:3], in_=hist[:]), reads=[B_hist], writes=[xp_b])
            else:
                P.op("dve", lambda e: e.memset(xp_t[:, :, 0:3], 0.0), writes=[xp_b])
            src = [(x_t[0:L, t, :], x_b) for t in range(nt)]
            first = (G["seq"]["kind"] == "prompt" and G["g"] == 0 and G["seq"]["b"] == 0)
            yield from stage_norm_T(src, gmixT, hT, B_hT, L)
            if first:
                dump("hT", hT[:], B_hT)
            yield from stage_inproj(G)
            yield from stage_inproj_tok(G)
            yield from stage_conv(G)
            yield from stage_xsB(G)

        def run_tail(G):
            T, L, nt = G["T"], G["L"], G["nt"]
            x_t, x_b = x1g[G["par"]]
            first = (G["seq"]["kind"] == "prompt" and G["g"] == 0 and G["seq"]["b"] == 0)
            def outnorm(t):
                yield from stage_outproj(G, [t])
                yield from stage_norm_T([(x_t[0:L, t, :], x_b)], gffnT, fT, B_fT, L, t0=t, scratch=[mix[t]])
            if nt > 1:
                drive([[stage_attn(G, 0), PSA(ATT[0]["psa"]), 2,
                        lambda: [[stage_attn(G, 1), PSA(ATT[1]["psa"]), 2], [outnorm(0), PSA([2, 3]), 2]]],
                       [stage_ssd(G, 1), PSA([4, 5]), 2]])
                last_on = outnorm(1)
            else:
                drive([[stage_attn(G, 0), PSA(ATT[0]["psa"]), 1]])
                last_on = outnorm(0)
            return last_on

        def load_x(G):
            x_t, x_b = x1g[G["par"]]
            for t in range(G["nt"]):
                P.dma("sp", x_t[0:G["L"], t, :], G["x_in"](t), writes=[x_b])

        groups = []
        par = 0

        def add_prompt_seq(b):
            nonlocal par
            seq = {"kind": "prompt", "has_state": False, "b": b}
            for g in range(8):
                G = {"seq": seq, "g": g, "T": 256, "L": 128, "nt": 2, "par": par, "ti0": 2 * g, "slot0": (2 * g) % 6,
                     "knf": [], "kv_out": None}
                G["x_in"] = (lambda t, b=b, g=g: AP(xp, (b * SEQ + g * 256 + t * 128) * D, [[D, 128], [1, D]]))
                G["y_out"] = (lambda t, b=b, g=g: AP(yp, (b * SEQ + g * 256 + t * 128) * D, [[D, 128], [1, D]]))
                if g >= 6:
                    G["kv_out"] = ((lambda t, b=b, g=g: AP(kpo, (b * 512 + (g - 6) * 256 + t * 128) * 512, [[512, 128], [1, 512]])),
                                   (lambda t, b=b, g=g: AP(vpo, (b * 512 + (g - 6) * 256 + t * 128) * 512, [[512, 128], [1, 512]])))
                groups.append(G)
                par ^= 1
        add_prompt_seq(0)
        seq_s = {"kind": "sample", "has_state": True, "b": 0}
        Gs = {"seq": seq_s, "g": 0, "T": 32, "L": 32, "nt": 1, "par": par, "ti0": 4, "slot0": 4, "knf": []}
        Gs["x_in"] = (lambda t: AP(xsm, 0, [[D, 32], [1, D]]))
        Gs["y_out"] = (lambda t: AP(ys, 0, [[D, 32], [1, D]]))
        Gs["kv_out"] = ((lambda t: AP(kso, 0, [[512, 32], [1, 512]])), (lambda t: AP(vso, 0, [[512, 32], [1, 512]])))
        groups.append(Gs)
        par ^= 1
        add_prompt_seq(1)

        def gen_sample_init(G):
            xp_t, xp_b = xpre[G["par"]]
            bk, bkb = ps()
            for half in range(2):
                o_t, o_b = ost.next()
                P.dma("sp", o_t[0:3, :], AP(sconv, half * 512, [[D, 3], [1, 512]]), writes=[o_b])

                def trc(e, bk=bk, o_t=o_t, half=half):
                    ins = None
                    for c in range(4):
                        cc = half * 4 + c
                        ins = e.transpose(out=bk[:, cc * 3:(cc + 1) * 3], in_=o_t[0:3, c * 128:(c + 1) * 128], identity=ident_f[0:3, 0:3])
                    return ins
                P.op("pe", trc, reads=[o_b, B_idf], writes=[bkb])
            P.op("act", lambda e, bk=bk, xp_t=xp_t: e.copy(out=xp_t[:, :, 0:3], in_=bk[:, 0:24].rearrange("p (c r) -> p c r", c=8)), reads=[bkb], writes=[xp_b])
            yield
            o_t, o_b = ost.next()
            P.dma("sp", o_t[:].rearrange("p (a n) -> p a n", a=4), AP(sssm, 0, [[128, 128], [128 * 128, 4], [1, 128]]), writes=[o_b])
            bk2, bk2b = ps()

            def trs(e, bk2=bk2, o_t=o_t):
                ins = None
                for hp in range(4):
                    ins = e.transpose(out=bk2[:, hp * 128:(hp + 1) * 128], in_=o_t[:, hp * 128:(hp + 1) * 128], identity=ident_f[:])
                return ins
            P.op("pe", trs, reads=[o_b, B_idf], writes=[bk2b])
            P.op("dve", lambda e, bk2=bk2: e.tensor_copy(out=state[:], in_=bk2[:]), reads=[bk2b], writes=[B_state])
            P.op("act", lambda e: e.copy(out=state_bf[:], in_=state[:]), reads=[B_state], writes=[B_statebf])
            yield
            for kt in range(4):
                P.dma("pool", xpost[:].rearrange("p c t -> p (c t)")[:, kt * 512:(kt + 1) * 512], AP(ck, kt * 128 * 512, [[512, 128], [1, 512]]), writes=[B_xpost])
                P.dma("pool", AP(Vr, kt * 520, [[6 * 520, 128], [65, 8], [1, 64]]), AP(cv, kt * 128 * 512, [[512, 128], [64, 8], [1, 64]]),
                      writes=[B_Vr])
            for hp in range(4):
                bk3, bk3b = ps()
                bkv = bk3[:].bitcast(BF16)

                def trk(e, bkv=bkv, hp=hp):
                    ins = None
                    for kt in range(4):
                        ins = e.transpose(out=bkv[:, kt * 128:(kt + 1) * 128], in_=xpost[:].rearrange("p c t -> p (c t)")[:, kt * 512 + hp * 128:kt * 512 + (hp + 1) * 128], identity=ident_bf[:])
                    return ins
                P.op("pe", trk, reads=[B_xpost, B_idb], writes=[bk3b])
                P.op("act", lambda e, bkv=bkv, hp=hp: e.copy(out=Kt[:, hp, 0:512], in_=bkv[:, 0:512]), reads=[bk3b], writes=[B_Kt])
                yield

        def prev_of(gi):
            if gi == 0 or groups[gi - 1]["seq"] is not groups[gi]["seq"]:
                return None
            return groups[gi - 1]

        try:
          _chk("setup")
          load_x(groups[0])
          drive([[chain(gen_head(groups[0], None), stage_ssd(groups[0], 0), stage_inproj_qk(groups[0])), PSA_ALL, 1]])
          for gi, G in enumerate(groups):
              if gi + 1 < len(groups):
                  load_x(groups[gi + 1])
              for jb0 in range(3):
                  load_ffn_gu(jb0)
                  load_ffn_dn(jb0)
              last_on = run_tail(G)
              last_of_seq = (gi + 1 == len(groups)) or (groups[gi + 1]["seq"] is not G["seq"])
              if last_of_seq:
                  if G["seq"]["kind"] == "prompt":
                      b = G["seq"]["b"]
                      emit_state_out(spo, b * 512 * 128)
                      emit_conv_out(G, cpo, b * 3 * D)
                  else:
                      emit_state_out(sso, 0)
                      emit_conv_out(G, cso, 0)
              items = [[chain(last_on, stage_ffn(G)), PSA([4, 5]), 1]]
              if gi + 1 < len(groups):
                  Gn = groups[gi + 1]
                  items.append([chain(gen_head(Gn, prev_of(gi + 1)), stage_ssd(Gn, 0), stage_inproj_qk(Gn)), PSA([6, 7]), 1])
              drive(items)
              _chk("g%d" % gi)
        except _Stop:
          pass

        P.wait_all("sp", out_toks)
        P.emit()
    return nc


_CACHE = {}


def kernel(x_prompt, x_sample, cache_attn_k, cache_attn_v, state_ssm, state_conv,
           norm_mix_g, w_in, conv_w, conv_b, dt_bias, a_log, d_skip, ssd_norm_g,
           q_norm_g, k_norm_g, rel_bias, w_out, norm_ffn_g, w_gate, w_up, w_down):
    f = lambda a: np.ascontiguousarray(np.asarray(a, dtype=np.float32))
    if "nc" not in _CACHE:
        _CACHE["nc"] = build_program()
    nc = _CACHE["nc"]
    x_prompt = f(x_prompt); x_sample = f(x_sample)
    ckk = f(cache_attn_k)[0].reshape(8, 512, 512)
    cvv = f(cache_attn_v)[0].reshape(8, 512, 512)
    sss = f(state_ssm)[0].reshape(8, 512, 128)
    scv = f(state_conv)[0]
    vecs = np.concatenate([f(conv_w)[0], f(conv_b), f(norm_mix_g), f(norm_ffn_g)], axis=0)
    small8 = np.concatenate([f(dt_bias), f(a_log), f(d_skip)], axis=0)
    qkg = np.concatenate([f(q_norm_g), f(k_norm_g)], axis=0)
    shared = {"w_in": f(w_in)[0], "w_out": f(w_out)[0], "w_gate": f(w_gate)[0], "w_up": f(w_up)[0], "w_down": f(w_down)[0],
              "vecs": f(vecs), "small8": f(small8), "ssdg": f(ssd_norm_g), "qkg": f(qkg), "relb": f(rel_bias)[0]}
    in_maps = []
    for c in range(NCORES):
        m = dict(shared)
        m.update({"xp": x_prompt[2 * c:2 * c + 2], "xsm": x_sample[c], "ck": ckk[c], "cv": cvv[c], "sssm": sss[c], "sconv": scv[c]})
        in_maps.append(m)
    res = run_bass_kernel_spmd(nc, in_maps, core_ids=list(range(NCORES)))
    R = res.results
    if DBG["dump"]:
        DBG["vals"] = {n: np.asarray(R[0][n]) for n in DBG["names"]}
    cat = lambda k: np.concatenate([np.asarray(r[k]) for r in R], axis=0)
    y_prompt = cat("yp").reshape(16, SEQ, D)
    y_sample = np.stack([np.asarray(r["ys"]) for r in R], axis=0)
    k_prompt = cat("kpo").reshape(1, 16, 512, 8, 64)
    v_prompt = cat("vpo").reshape(1, 16, 512, 8, 64)
    ssm_prompt = cat("spo").reshape(1, 16, 8, 64, 128)
    conv_prompt = cat("cpo").reshape(1, 16, 3, D)
    k_sample = np.stack([np.asarray(r["kso"]) for r in R], axis=0).reshape(1, 8, 32, 8, 64)
    v_sample = np.stack([np.asarray(r["vso"]) for r in R], axis=0).reshape(1, 8, 32, 8, 64)
    ssm_sample = np.stack([np.asarray(r["sso"]) for r in R], axis=0).reshape(1, 8, 8, 64, 128)
    conv_sample = np.stack([np.asarray(r["cso"]) for r in R], axis=0).reshape(1, 8, 3, D)
    return tuple(np.ascontiguousarray(a, dtype=np.float32) for a in
                 (y_prompt, y_sample, k_prompt, v_prompt, ssm_prompt, conv_prompt, k_sample, v_sample, ssm_sample, conv_sample))
```

```python
import contextlib
import numpy as np
import concourse.bass as bass
import concourse.mybir as mybir
from concourse.bass_utils import run_bass_kernel_spmd
from concourse.ap import AP

F32 = mybir.dt.float32
BF16 = mybir.dt.bfloat16
AF = mybir.ActivationFunctionType
ALU = mybir.AluOpType

NCORES = 8
D = 1024
SEQ = 2048
NPROJ = 3080
DFF = 2816
NJB = 22
EPS = 1e-6
C_Z, C_XBC, C_DT, C_Q, C_K, C_V = 0, 512, 1536, 1544, 2056, 2568


class Buf:
    __slots__ = ("name", "w", "r", "excl")

    def __init__(self, name, excl=False):
        self.name = name
        self.w = None
        self.r = []
        self.excl = excl


class Prog:
    ENGS = ("pe", "act", "dve", "pool", "sp")

    def __init__(self, nc):
        self.nc = nc
        self.ops = {e: [] for e in self.ENGS}
        self.cnt = {e: 0 for e in self.ENGS}
        self.known = {e: {} for e in self.ENGS}
        self.same = {"act", "dve", "pool"}
        self.dma_keys = []
        self.sems = {}

    def dma_sem(self, key):
        self.cnt[key] = 0
        self.dma_keys.append(key)

    def _deps(self, eng, reads, writes):
        deps = {}

        def add(tok):
            if tok is None:
                return
            k, v = tok
            if k == eng and eng not in self.same:
                return
            if deps.get(k, 0) < v:
                deps[k] = v
        for b in reads:
            add(b.w)
            if b.excl:
                for t in b.r:
                    if t[0] != eng:
                        add(t)
        for b in writes:
            add(b.w)
            for t in b.r:
                if t[0] == eng:
                    continue
                add(t)
        waits = []
        kn = self.known[eng]
        for k, v in deps.items():
            if kn.get(k, 0) < v:
                kn[k] = v
                waits.append((k, v))
        return waits

    def _commit(self, tok, reads, writes):
        for b in reads:
            b.r.append(tok)
        for b in writes:
            b.w = tok
            b.r = []

    def op(self, eng, fn, reads=(), writes=()):
        waits = self._deps(eng, reads, writes)
        self.cnt[eng] += 1
        tok = (eng, self.cnt[eng])
        self.ops[eng].append((waits, fn, (eng, 1)))
        self._commit(tok, reads, writes)
        return tok

    def dma(self, q, out, in_, reads=(), writes=(), store=False, **kw):
        if writes and not store:
            key = "L_" + writes[0].name
        else:
            key = "S_" + reads[0].name
        if key not in self.cnt:
            self.cnt[key] = 0
            self.dma_keys.append(key)
        waits = self._deps(q, reads, writes)
        self.cnt[key] += 16
        tok = (key, self.cnt[key])

        def fn(e, out=out, in_=in_, kw=kw):
            return e.dma_start(out=out, in_=in_, **kw)
        self.ops[q].append((waits, fn, (key, 16)))
        self._commit(tok, reads, writes)
        return tok

    def barrier(self, skip_prefix=None):
        toks = [(k, v) for k, v in self.cnt.items() if v > 0 and not (skip_prefix and k.startswith(skip_prefix))]
        for e in self.ENGS:
            self.wait_all(e, toks)

    def wait_all(self, eng, toks):
        waits = []
        kn = self.known[eng]
        for k, v in toks:
            if kn.get(k, 0) < v:
                kn[k] = v
                waits.append((k, v))
        self.ops[eng].append((waits, None, None))

    def emit(self):
        nc = self.nc
        with contextlib.ExitStack() as st:
            for k in list(self.ENGS) + self.dma_keys:
                self.sems[k] = st.enter_context(nc.semaphore("s_" + k))
            block = st.enter_context(nc.Block())
            hooks = {"pe": block.tensor, "act": block.scalar, "dve": block.vector,
                     "pool": block.gpsimd, "sp": block.sync}
            for e in self.ENGS:
                ops = self.ops[e]

                def run(engine, ops=ops):
                    for waits, fn, inc in ops:
                        for k, v in waits:
                            engine.wait_ge(self.sems[k], v)
                        if fn is not None:
                            ins = fn(engine)
                            ins.then_inc(self.sems[inc[0]], inc[1])
                hooks[e](run)


class Rot:
    def __init__(self, items):
        self.items = items
        self.i = 0

    def next(self):
        it = self.items[self.i % len(self.items)]
        self.i += 1
        return it


def bc3(ap2, n):
    return ap2.unsqueeze(2).to_broadcast([ap2.shape[0], ap2.shape[1], n])


class _Stop(Exception):
    pass


DBG = {"stop": None, "dump": False, "names": [], "kv": "both"}


def _chk(name):
    if DBG["stop"] == name:
        raise _Stop()


def build_program():
    nc = bass.Bass("TRN2", target_bir_lowering=False)
    P = Prog(nc)

    def din(name, shape):
        return nc.dram_tensor(name, shape, F32, kind="ExternalInput")

    def dout(name, shape):
        return nc.dram_tensor(name, shape, F32, kind="ExternalOutput")

    xp = din("xp", [2, SEQ, D])
    xsm = din("xsm", [32, D])
    ck = din("ck", [512, 512])
    cv = din("cv", [512, 512])
    sssm = din("sssm", [512, 128])
    sconv = din("sconv", [3, D])
    w_in = din("w_in", [D, NPROJ])
    w_out = din("w_out", [D, D])
    w_gate = din("w_gate", [D, DFF])
    w_up = din("w_up", [D, DFF])
    w_down = din("w_down", [DFF, D])
    vecs = din("vecs", [7, D])
    small8 = din("small8", [3, 8])
    ssdg = din("ssdg", [1, 512])
    qkg = din("qkg", [2, 64])
    relb = din("relb", [8, 257])

    yp = dout("yp", [2, SEQ, D])
    ys = dout("ys", [32, D])
    kpo = dout("kpo", [2, 512, 512])
    vpo = dout("vpo", [2, 512, 512])
    spo = dout("spo", [2, 512, 128])
    cpo = dout("cpo", [2, 3, D])
    kso = dout("kso", [32, 512])
    vso = dout("vso", [32, 512])
    sso = dout("sso", [512, 128])
    cso = dout("cso", [3, D])

    text = nc.dram_tensor("text", [8, 128, 768], F32, kind="Internal")
    text1 = nc.dram_tensor("text1", [8, 768], F32, kind="Internal")
    B_text1 = Buf("text1")
    wbs = nc.dram_tensor("wbs", [NJB, 128, 3072], BF16, kind="Internal")
    B_text, B_wbs = Buf("text"), Buf("wbs")

    est = contextlib.ExitStack()
    with est:
        def sb(name, shape, dt):
            return est.enter_context(nc.sbuf_tensor(name, shape, dt)), Buf(name)

        win, B_win = sb("win", [128, 8, NPROJ], BF16)
        wout, B_wout = sb("wout", [128, 8, D], BF16)
        wgu = [sb("wgu%d" % i, [128, 2048], BF16) for i in range(3)]
        wdn = [sb("wdn%d" % i, [128, 1024], BF16) for i in range(3)]
        ident_bf, B_idb = sb("ident_bf", [128, 128], BF16)
        ident_f, B_idf = sb("ident_f", [128, 128], F32)
        tri_f, B_tri = sb("tri_f", [128, 128], F32)
        ones_f, B_ones = sb("ones_f", [128, 128], F32)
        maskneg, B_mneg = sb("maskneg", [128, 128], F32)
        blk_bf, B_blk = sb("blk_bf", [128, 128], BF16)
        vecT, B_vecT = sb("vecT", [128, 8, 7], F32)
        gq2, B_gq = sb("gq2", [128, 1], F32)
        gk2, B_gk = sb("gk2", [128, 1], F32)
        s8bc, B_s8 = sb("s8bc", [128, 3, 8], F32)
        ssdg_bc, B_ssdg = sb("ssdg_bc", [128, 512], F32)
        eps_c, B_eps = sb("eps_c", [128, 1], F32)
        one_c, B_one = sb("one_c", [128, 1], F32)
        tab_e, B_tab = sb("tab_e", [128, 8, 640], BF16)
        Kt, B_Kt = sb("Kt", [128, 4, 768], BF16)
        Vr, B_Vr = sb("Vr", [128, 6, 8 * 65], BF16)
        state, B_state = sb("state", [128, 512], F32)
        state_bf, B_statebf = sb("state_bf", [128, 512], BF16)

        banks = []
        for i in range(8):
            t = est.enter_context(nc.psum_tensor("pb%d" % i, [128, 512], F32))
            banks.append((t, Buf("pb%d" % i, excl=True)))
        class PSA:
            def __init__(self, idx):
                self.idx = list(idx)
                self.i = 0

            def next(self):
                b = banks[self.idx[self.i % len(self.idx)]]
                self.i += 1
                return b
        PSA_ALL = PSA(range(8))
        ps_state = {"cur": PSA_ALL}

        def ps():
            return ps_state["cur"].next()

        def drive(items):
            items = [list(it) for it in items]
            while items:
                for it in list(items):
                    gen, psa, w = it[0], it[1], it[2]
                    ps_state["cur"] = psa
                    try:
                        for _ in range(w):
                            next(gen)
                    except StopIteration:
                        items.remove(it)
                        if len(it) > 3:
                            items.extend([list(x) for x in it[3]()])
            ps_state["cur"] = PSA_ALL

        def chain(*gens):
            for g in gens:
                yield from g

        P.op("pool", lambda e: e.memset(ident_f[:], 0.0), writes=[B_idf])
        P.op("pool", lambda e: e.affine_select(out=ident_f[:], in_=ident_f[:], pattern=[[-1, 128]],
                                               compare_op=ALU.not_equal, fill=1.0, base=0, channel_multiplier=1),
             reads=[B_idf], writes=[B_idf])
        P.op("dve", lambda e: e.tensor_copy(out=ident_bf[:], in_=ident_f[:]), reads=[B_idf], writes=[B_idb])
        P.op("pool", lambda e: e.memset(ones_f[:], 1.0), writes=[B_ones])
        P.op("pool", lambda e: e.affine_select(out=tri_f[:], in_=ones_f[:], pattern=[[1, 128]],
                                               compare_op=ALU.is_ge, fill=0.0, base=0, channel_multiplier=-1),
             reads=[B_ones], writes=[B_tri])
        P.op("pool", lambda e: e.memset(maskneg[:], 0.0), writes=[B_mneg])
        P.op("pool", lambda e: e.affine_select(out=maskneg[:], in_=maskneg[:], pattern=[[1, 128]],
                                               compare_op=ALU.is_ge, fill=-30000.0, base=0, channel_multiplier=-1),
             reads=[B_mneg], writes=[B_mneg])
        P.op("pool", lambda e: e.memset(blk_bf[:], 0.0), writes=[B_blk])
        P.op("pool", lambda e: e.memset(blk_bf[0:64, 0:64], 1.0), writes=[B_blk])
        P.op("pool", lambda e: e.memset(blk_bf[64:128, 64:128], 1.0), writes=[B_blk])
        P.op("pool", lambda e: e.memset(eps_c[:], EPS), writes=[B_eps])
        P.op("pool", lambda e: e.memset(one_c[:], 1.0), writes=[B_one])
        P.op("pool", lambda e: e.memset(Vr[:], 1.0), writes=[B_Vr])
        P.op("pool", lambda e: e.memset(Kt[:], 0.0), writes=[B_Kt])
        B_blk_w = [Buf("wbs%d" % jb) for jb in range(NJB)]
        for jb in range(NJB):
            P.dma("pool", AP(wbs, jb * 128 * 3072, [[3072, 128], [128, 8], [1, 128]]), AP(w_gate, jb * 128, [[DFF, 128], [128 * DFF, 8], [1, 128]]), writes=[B_blk_w[jb]])
            P.dma("pool", AP(wbs, jb * 128 * 3072 + 1024, [[3072, 128], [128, 8], [1, 128]]), AP(w_up, jb * 128, [[DFF, 128], [128 * DFF, 8], [1, 128]]), writes=[B_blk_w[jb]])
            P.dma("pool", AP(wbs, jb * 128 * 3072 + 2048, [[3072, 128], [1, 1024]]), AP(w_down, jb * 128 * D, [[D, 128], [1, D]]), writes=[B_blk_w[jb]])

        P.dma("sp", s8bc[:], AP(small8, 0, [[0, 128], [8, 3], [1, 8]]), writes=[B_s8])
        P.dma("sp", ssdg_bc[:], AP(ssdg, 0, [[0, 128], [1, 512]]), writes=[B_ssdg])
        P.dma("sp", gq2[0:64, :], AP(qkg, 0, [[1, 64], [1, 1]]), writes=[B_gq])
        P.dma("sp", gq2[64:128, :], AP(qkg, 0, [[1, 64], [1, 1]]), writes=[B_gq])
        P.dma("sp", gk2[0:64, :], AP(qkg, 64, [[1, 64], [1, 1]]), writes=[B_gk])
        P.dma("sp", gk2[64:128, :], AP(qkg, 64, [[1, 64], [1, 1]]), writes=[B_gk])
        P.op("dve", lambda e: e.tensor_scalar(out=gq2[:], in0=gq2[:], scalar1=0.125, scalar2=None, op0=ALU.mult),
             reads=[B_gq], writes=[B_gq])
        P.op("act", lambda e: e.activation(out=s8bc[:, 1, :], in_=s8bc[:, 1, :], func=AF.Exp), reads=[B_s8], writes=[B_s8])
        P.op("dve", lambda e: e.tensor_scalar(out=s8bc[:, 1, :], in0=s8bc[:, 1, :], scalar1=-1.0, scalar2=None, op0=ALU.mult),
             reads=[B_s8], writes=[B_s8])
        dtb_bc = s8bc[:, 0, :]
        a_bc = s8bc[:, 1, :]
        dsk_bc = s8bc[:, 2, :]

        with contextlib.ExitStack() as tst:
            vrow = tst.enter_context(nc.sbuf_tensor("vrow", [7, D], F32))
            B_vrow = Buf("vrow")
            textsb = tst.enter_context(nc.sbuf_tensor("textsb", [8, 768], F32))
            B_textsb = Buf("textsb")

            P.dma("sp", vrow[:], AP(vecs, 0, [[D, 7], [1, D]]), writes=[B_vrow])
            bk, B_bk = ps()

            def tr_vec(e):
                ins = None
                for c in range(8):
                    ins = e.transpose(out=bk[:, c * 7:(c + 1) * 7], in_=vrow[0:7, c * 128:(c + 1) * 128], identity=ident_f[0:7, 0:7])
                return ins
            P.op("pe", tr_vec, reads=[B_vrow, B_idf], writes=[B_bk])
            P.op("act", lambda e: e.copy(out=vecT[:].rearrange("p c r -> p (c r)"), in_=bk[:, 0:56]), reads=[B_bk], writes=[B_vecT])

            NSTG = 4
            stg = [(tst.enter_context(nc.sbuf_tensor("stg%d" % i, [128, NPROJ], F32)), Buf("stg%d" % i)) for i in range(NSTG)]
            cast_i = [0]

            def cast(out_ap, in_ap, reads, writes):
                eng = ("act", "dve")[cast_i[0] % 2]
                cast_i[0] += 1
                if eng == "act":
                    P.op("act", lambda e: e.copy(out=out_ap, in_=in_ap), reads=reads, writes=writes)
                else:
                    P.op(eng, lambda e: e.tensor_copy(out=out_ap, in_=in_ap), reads=reads, writes=writes)
            si = 0
            for k in range(8):
                st_t, st_b = stg[si % NSTG]; si += 1
                P.dma("sp", st_t[:, 0:NPROJ], AP(w_in, k * 128 * NPROJ, [[NPROJ, 128], [1, NPROJ]]), writes=[st_b])
                cast(win[:, k, :], st_t[:, 0:NPROJ], [st_b], [B_win])
            for k in range(8):
                st_t, st_b = stg[si % NSTG]; si += 1
                P.dma("sp", st_t[:, 0:D], AP(w_out, k * 128 * D, [[D, 128], [1, D]]), writes=[st_b])
                cast(wout[:, k, :], st_t[:, 0:D], [st_b], [B_wout])
            P.dma("sp", textsb[:, 0:257], AP(relb, 0, [[257, 8], [1, 257]]), writes=[B_textsb])
            P.op("dve", lambda e: e.tensor_copy(out=textsb[:, 257:768], in_=textsb[:, 256:257].to_broadcast([8, 511])),
                 reads=[B_textsb], writes=[B_textsb])
            P.dma("sp", AP(text1, 0, [[768, 8], [1, 768]]), textsb[:], reads=[B_textsb], writes=[B_text1], store=True)
            text128 = tst.enter_context(nc.sbuf_tensor("text128", [128, 8, 768], F32))
            B_text128 = Buf("text128")
            P.dma("sp", AP(text, 0, [[768, 128], [128 * 768, 8], [1, 768]]), AP(text1, 0, [[0, 128], [768, 8], [1, 768]]),
                  reads=[B_text1], writes=[B_text])
            tabf, B_tabf = text128[:, :, 0:640], B_text128
            P.dma("sp", tabf, AP(text, 128, [[767, 128], [128 * 768, 8], [1, 640]]), reads=[B_text], writes=[B_tabf])
            P.op("act", lambda e: e.activation(out=tab_e[:], in_=tabf, func=AF.Exp), reads=[B_tabf], writes=[B_tab])
            P.op("dve", lambda e: e.memset(tab_e[0:64, :, 576:640], 0.0), writes=[B_tab])
            P.op("dve", lambda e: e.memset(tab_e[64:128, :, 0:64], 0.0), writes=[B_tab])

            P.barrier(skip_prefix="L_wbs")

        x1g = [sb("x1g%d" % i, [128, 2, D], F32) for i in range(2)]
        xsc = Rot([sb("xsc%d" % i, [128, D], BF16) for i in range(2)])
        stats = Rot([sb("stat%d" % i, [128, 4], F32) for i in range(10)])
        hT, B_hT = sb("hT", [128, 8, 256], BF16)
        fT, B_fT = sb("fT", [128, 8, 256], BF16)
        _xp = sb("xpre", [128, 8, 3 + 256], BF16)
        xpre = [_xp, _xp]
        hist, B_hist = sb("hist", [128, 8, 3], BF16)
        xpost, B_xpost = sb("xpost", [128, 8, 256], BF16)
        cacc = Rot([sb("cacc%d" % i, [128, 256], F32) for i in range(3)])
        rawf = Rot([sb("rawf%d" % i, [128, 256], F32) for i in range(2)])
        sqb = Rot([sb("sqb%d" % i, [128, 256], BF16) for i in range(2)])
        sdf = Rot([sb("sdf%d" % i, [128, 256], F32) for i in range(2)])
        knf = Rot([sb("knf%d" % i, [128, 256], F32) for i in range(2)])
        qT, B_qT = sb("qT", [128, 8, 256], BF16)
        zs = [sb("zs%d" % i, [128, 512], BF16) for i in range(2)]
        dtt = [sb("dtt%d" % i, [128, 8], F32) for i in range(2)]
        sm8 = Rot([sb("sm8_%d" % i, [128, 8], F32) for i in range(8)])
        cdec, B_cdec = sb("cdec", [128, 8], F32)
        xsB = [sb("xsB%d" % i, [128, 768], BF16) for i in range(2)]
        rhs1, B_rhs1 = sb("rhs1", [128, 8, 128], F32)
        _r1 = rhs1[:].rearrange("p h q -> p (h q)")
        kst = [(_r1[:, 0:512], B_rhs1), (_r1[:, 512:1024], B_rhs1)]
        segf = Rot([sb("segf%d" % i, [128, 4, 128], F32) for i in range(1)])
        dec = Rot([sb("dec%d" % i, [128, 4, 128], BF16) for i in range(2)])
        Mt, B_Mt = sb("Mt", [128, 8, 128], BF16)
        ebc, B_ebc = sb("ebc", [128, 8, 128], BF16)
        Cp, B_Cp = sb("Cp", [128, 8, 128], BF16)
        xdt, B_xdt = sb("xdt", [128, 8, 64], BF16)
        xdte, B_xdte = sb("xdte", [128, 8, 64], BF16)
        ytmp = Rot([sb("ytmp%d" % i, [128, 512], F32) for i in range(1)])
        stmp, B_stmp = ytmp.items[0]
        _eb = [sb("ebuf%d" % i, [128, 4, 128], BF16) for i in range(2)]
        _pt = [sb("PT%d" % i, [128, 4, 128], BF16) for i in range(4)]
        _ebr, _ptr = Rot(_eb), Rot(_pt)
        ATT = [{"ob": (0, 1), "psa": [2, 3], "ebuf": _ebr, "PT": _ptr},
               {"ob": (6, 7), "psa": [0, 1], "ebuf": _ebr, "PT": _ptr}]
        rec = Rot([sb("rec%d" % i, [128, 4], F32) for i in range(2)])
        mix = [sb("mix%d" % i, [128, D], BF16) for i in range(2)]
        mixT = Rot([sb("mixT%d" % i, [128, 8, 128], BF16) for i in range(2)])
        sg = Rot([sb("sg%d" % i, [128, 256], F32) for i in range(2)])
        aT = Rot([sb("aT%d" % i, [128, 256], BF16) for i in range(3)])
        ost = Rot([sb("ost%d" % i, [128, 512], F32) for i in range(2)])

        P.op("dve", lambda e: e.memset(qT[:], 0.0), writes=[B_qT])
        gmixT = AP(vecT, 5, [[56, 128], [7, 8]])
        gffnT = AP(vecT, 6, [[56, 128], [7, 8]])

        def cw_ap(c, tap):
            return vecT[:, c, tap:tap + 1]

        out_toks = []

        def dump(name, ap, buf):
            if not DBG["dump"]:
                return
            dt = nc.dram_tensor("dbg_" + name, list(ap.shape), ap.dtype, kind="ExternalOutput")
            DBG["names"].append("dbg_" + name)
            out_toks.append(P.dma("sp", dt.ap(), ap, reads=[buf]))

        def act_sigmoid(dst_ap, src_ap, reads, dst_b):
            P.op("act", lambda e: e.activation(out=dst_ap, in_=src_ap, func=AF.Exp, scale=-1.0), reads=reads, writes=[dst_b])
            P.op("act", lambda e: e.activation(out=dst_ap, in_=dst_ap, func=AF.Ln, bias=one_c[0:dst_ap.shape[0], :]), reads=[dst_b, B_one], writes=[dst_b])
            P.op("act", lambda e: e.activation(out=dst_ap, in_=dst_ap, func=AF.Exp, scale=-1.0), reads=[dst_b], writes=[dst_b])

        def rstd_of(ss_ap, st_t, st_b, L, n):
            P.op("act", lambda e: e.activation(out=st_t[0:L, 1:2], in_=st_t[0:L, 0:1], func=AF.Ln, scale=1.0 / n, bias=eps_c[0:L, :]),
                 reads=[st_b, B_eps], writes=[st_b])
            P.op("act", lambda e: e.activation(out=st_t[0:L, 2:3], in_=st_t[0:L, 1:2], func=AF.Exp, scale=-0.5), reads=[st_b], writes=[st_b])

        def stage_norm_T(src, gT, dst_t, dst_b, L, t0=0, scratch=None):
            scaled = []
            for t, (xap, xb) in enumerate(src):
                st_t, st_b = stats.next()
                xs_t, xs_b = scratch[t] if scratch is not None else xsc.next()
                P.op("act", lambda e, xap=xap, st_t=st_t, xs_t=xs_t: e.activation(out=xs_t[0:L, :], in_=xap, func=AF.Square, accum_out=st_t[0:L, 0:1]),
                     reads=[xb], writes=[xs_b, st_b])
                scaled.append((xs_t, xs_b, st_t, st_b, xap, xb))
            yield
            for (xs_t, xs_b, st_t, st_b, xap, xb) in scaled:
                rstd_of(None, st_t, st_b, L, D)
            yield
            for (xs_t, xs_b, st_t, st_b, xap, xb) in scaled:
                P.op("dve", lambda e, xap=xap, st_t=st_t, xs_t=xs_t: e.tensor_scalar(out=xs_t[0:L, :], in0=xap, scalar1=st_t[0:L, 2:3], scalar2=None, op0=ALU.mult),
                     reads=[xb, st_b], writes=[xs_b])
            yield
            for t, (xs_t, xs_b, st_t, st_b, xap, xb) in enumerate(scaled, start=t0):
                bk, bkb = ps()
                bkv = bk[:].bitcast(BF16)

                def tr(e, xs_t=xs_t, bkv=bkv):
                    ins = None
                    for k in range(8):
                        ins = e.transpose(out=bkv[:, k * L:(k + 1) * L], in_=xs_t[0:L, k * 128:(k + 1) * 128], identity=ident_bf[0:L, 0:L])
                    return ins
                P.op("pe", tr, reads=[xs_b, B_idb], writes=[bkb])
                P.op("dve", lambda e, bkv=bkv, t=t: e.tensor_tensor(out=dst_t[:, :, t * L:(t + 1) * L],
                                                                   in0=bkv[:, 0:8 * L].rearrange("p (k l) -> p k l", k=8),
                                                                   in1=bc3(gT, L), op=ALU.mult),
                     reads=[bkb, B_vecT], writes=[dst_b])
                yield

        def fm_mm(col0, T):
            bk, bkb = ps()

            def mm(e, bk=bk):
                ins = None
                for k in range(8):
                    ins = e.matmul(bk[:, 0:T], lhsT=win[:, k, col0:col0 + 128], rhs=hT[:, k, 0:T], start=(k == 0), stop=(k == 7))
                return ins
            P.op("pe", mm, reads=[B_win, B_hT], writes=[bkb])
            return bk, bkb

        def stage_inproj(G):
            T, L, nt = G["T"], G["L"], G["nt"]
            xp_t, xp_b = xpre[G["par"]]
            for c in range(8):
                bk, bkb = fm_mm(C_XBC + 128 * c, T)
                P.op("act", lambda e, bk=bk, c=c: e.copy(out=xp_t[:, c, 3:3 + T], in_=bk[:, 0:T]), reads=[bkb], writes=[xp_b])
                yield

        def stage_inproj_qk(G):
            T, L, nt = G["T"], G["L"], G["nt"]
            def qk_store(which, hp, rw_t, rw_b, sd_t, sd_b):
                if which == "q":
                    P.op("dve", lambda e: e.scalar_tensor_tensor(
                        out=qT[0:64, 2 * hp, 0:T], in0=rw_t[0:64, 0:T], scalar=gq2[0:64, 0:1], in1=sd_t[0:64, 0:T], op0=ALU.mult, op1=ALU.mult),
                        reads=[rw_b, sd_b, B_gq], writes=[B_qT])
                    P.op("dve", lambda e: e.scalar_tensor_tensor(
                        out=qT[64:128, 2 * hp + 1, 0:T], in0=rw_t[64:128, 0:T], scalar=gq2[64:128, 0:1], in1=sd_t[64:128, 0:T], op0=ALU.mult, op1=ALU.mult),
                        reads=[rw_b, sd_b, B_gq], writes=[B_qT])
                else:
                    c0 = G["slot0"] * 128
                    P.op("dve", lambda e: e.scalar_tensor_tensor(
                        out=Kt[:, hp, c0:c0 + T], in0=rw_t[:, 0:T], scalar=gk2[:, 0:1], in1=sd_t[:, 0:T], op0=ALU.mult, op1=ALU.mult),
                        reads=[rw_b, sd_b, B_gk], writes=[B_Kt])
                    if G["kv_out"] is not None and DBG["kv"] in ("both", "k"):
                        kn_t, kn_b = knf.next()
                        P.op("dve", lambda e: e.scalar_tensor_tensor(
                            out=kn_t[:, 0:T], in0=rw_t[:, 0:T], scalar=gk2[:, 0:1], in1=sd_t[:, 0:T], op0=ALU.mult, op1=ALU.mult),
                            reads=[rw_b, sd_b, B_gk], writes=[kn_b])
                        for t in range(nt):
                            bkk, bkkb = ps()
                            P.op("pe", lambda e, bkk=bkk, t=t: e.transpose(out=bkk[0:L, 0:128], in_=kn_t[:, t * L:(t + 1) * L], identity=ident_f[:]),
                                 reads=[kn_b, B_idf], writes=[bkkb])
                            ks_t, ks_b = kst[t]
                            P.op("act", lambda e, bkk=bkk, ks_t=ks_t: e.copy(out=ks_t[0:L, hp * 128:(hp + 1) * 128], in_=bkk[0:L, 0:128]),
                                 reads=[bkkb], writes=[ks_b])

            chunks = [(which, hp) for which in ("q", "k") for hp in range(4)]
            p_cs, p_blk, p_ln, p_st = [], [], [], []
            for c in range(len(chunks) + 2):
                if p_st:
                    qk_store(*p_st.pop(0))
                if c < len(chunks):
                    which, hp = chunks[c]
                    col0 = (C_Q if which == "q" else C_K) + 128 * hp
                    bk, bkb = fm_mm(col0, T)
                    p_cs.append((which, hp, bk, bkb))
                if p_blk:
                    (which1, hp1, rw_t, rw_b, sq_t, sq_b) = p_blk.pop(0)
                    bk2, bk2b = ps()
                    P.op("pe", lambda e, bk2=bk2, sq_t=sq_t: e.matmul(bk2[:, 0:T], lhsT=blk_bf[:], rhs=sq_t[:, 0:T], start=True, stop=True),
                         reads=[sq_b, B_blk], writes=[bk2b])
                    p_ln.append((which1, hp1, rw_t, rw_b, bk2, bk2b))
                yield
                if p_cs:
                    (which0, hp0, bk0, bk0b) = p_cs.pop(0)
                    rw_t, rw_b = rawf.next()
                    P.op("act", lambda e, bk0=bk0, rw_t=rw_t: e.copy(out=rw_t[:, 0:T], in_=bk0[:, 0:T]), reads=[bk0b], writes=[rw_b])
                    sq_t, sq_b = sqb.next()
                    P.op("act", lambda e, bk0=bk0, sq_t=sq_t: e.activation(out=sq_t[:, 0:T], in_=bk0[:, 0:T], func=AF.Square),
                         reads=[bk0b], writes=[sq_b])
                    p_blk.append((which0, hp0, rw_t, rw_b, sq_t, sq_b))
                if p_ln:
                    (which1, hp1, rw1_t, rw1_b, bk2, bk2b) = p_ln.pop(0)
                    sd_t, sd_b = sdf.next()
                    P.op("act", lambda e, bk2=bk2, sd_t=sd_t: e.activation(out=sd_t[:, 0:T], in_=bk2[:, 0:T], func=AF.Ln, scale=1.0 / 64, bias=eps_c[:]),
                         reads=[bk2b, B_eps], writes=[sd_b])
                    P.op("act", lambda e, sd_t=sd_t: e.activation(out=sd_t[:, 0:T], in_=sd_t[:, 0:T], func=AF.Exp, scale=-0.5), reads=[sd_b], writes=[sd_b])
                    p_st.append((which1, hp1, rw1_t, rw1_b, sd_t, sd_b))
                yield
            while p_st:
                qk_store(*p_st.pop(0))
            yield
            if G["kv_out"] is not None and DBG["kv"] in ("both", "k"):
                for t in range(nt):
                    ks_t, ks_b = kst[t]
                    out_toks.append(P.dma("sp", G["kv_out"][0](t), ks_t[0:L, :], reads=[ks_b]))

        def stage_inproj_tok(G):
            T, L, nt = G["T"], G["L"], G["nt"]
            for t in range(nt):
                cols = slice(t * L, (t + 1) * L)
                bk, bkb = ps()

                def mmz(e, bk=bk, cols=cols):
                    ins = None
                    for k in range(8):
                        ins = e.matmul(bk[0:L, :], lhsT=hT[:, k, cols], rhs=win[:, k, C_Z:C_Z + 512], start=(k == 0), stop=(k == 7))
                    return ins
                P.op("pe", mmz, reads=[B_win, B_hT], writes=[bkb])
                z_t, z_b = zs[t]
                zt_t, zt_b = ytmp.items[0]
                bkz, bkzb = bk, bkb
                yield
                act_sigmoid(zt_t[0:L, :], bkz[0:L, :], [bkzb], zt_b)
                bk, bkb = ps()

                def mmv(e, bk=bk, cols=cols):
                    ins = None
                    for k in range(8):
                        ins = e.matmul(bk[0:L, :], lhsT=hT[:, k, cols], rhs=win[:, k, C_V:C_V + 512], start=(k == 0), stop=(k == 7))
                    return ins
                P.op("pe", mmv, reads=[B_win, B_hT], writes=[bkb])
                yield
                P.op("dve", lambda e, bkz=bkz, z_t=z_t, zt_t=zt_t: e.tensor_tensor(out=z_t[0:L, :], in0=zt_t[0:L, :], in1=bkz[0:L, :], op=ALU.mult),
                     reads=[bkzb, zt_b], writes=[z_b])
                slot = G["slot0"] + t
                vdst = AP(Vr, slot * 520, [[6 * 520, L], [65, 8], [1, 64]])
                P.op("act", lambda e, bk=bk, vdst=vdst: e.copy(out=vdst, in_=bk[0:L, :].rearrange("p (h d) -> p h d", h=8)),
                     reads=[bkb], writes=[B_Vr])
                if G["kv_out"] is not None and DBG["kv"] in ("both", "v"):
                    o_t, o_b = ost.next()
                    P.op("dve", lambda e, bk=bk, o_t=o_t: e.tensor_copy(out=o_t[0:L, :], in_=bk[0:L, :]), reads=[bkb], writes=[o_b])
                    out_toks.append(P.dma("sp", G["kv_out"][1](t), o_t[0:L, :], reads=[o_b]))
                yield
                bk, bkb = ps()

                def mmd(e, bk=bk, cols=cols):
                    ins = None
                    for k in range(8):
                        ins = e.matmul(bk[0:L, 0:8], lhsT=hT[:, k, cols], rhs=win[:, k, C_DT:C_DT + 8], start=(k == 0), stop=(k == 7))
                    return ins
                P.op("pe", mmd, reads=[B_win, B_hT], writes=[bkb])
                d_t, d_b = dtt[t]
                yield
                P.op("dve", lambda e, bk=bk, d_t=d_t: e.tensor_tensor(out=d_t[0:L, :], in0=bk[0:L, 0:8], in1=dtb_bc[0:L, :], op=ALU.add),
                     reads=[bkb, B_s8], writes=[d_b])
                yield
                P.op("act", lambda e, d_t=d_t: e.activation(out=d_t[0:L, :], in_=d_t[0:L, :], func=AF.Exp), reads=[d_b], writes=[d_b])
                P.op("act", lambda e, d_t=d_t: e.activation(out=d_t[0:L, :], in_=d_t[0:L, :], func=AF.Ln, bias=one_c[0:L, :]), reads=[d_b, B_one], writes=[d_b])
                yield

        def stage_conv(G):
            T = G["T"]
            xp_t, xp_b = xpre[G["par"]]
            s1, s2 = [], []
            for c in range(8 + 2):
                if s2:
                    (c2, ca_t, ca_b, sgm_t, sgm_b) = s2.pop(0)
                    P.op("dve", lambda e, c2=c2, ca_t=ca_t, sgm_t=sgm_t: e.tensor_tensor(out=xpost[:, c2, 0:T], in0=ca_t[:, 0:T], in1=sgm_t[:, 0:T], op=ALU.mult),
                         reads=[ca_b, sgm_b], writes=[B_xpost])
                if s1:
                    (c1, ca_t, ca_b) = s1.pop(0)
                    sgm_t, sgm_b = sdf.next()
                    act_sigmoid(sgm_t[:, 0:T], ca_t[:, 0:T], [ca_b], sgm_b)
                    s2.append((c1, ca_t, ca_b, sgm_t, sgm_b))
                if c < 8:
                    ca_t, ca_b = cacc.next()
                    P.op("dve", lambda e, c=c, ca_t=ca_t: e.tensor_scalar(out=ca_t[:, 0:T], in0=xp_t[:, c, 0:T], scalar1=cw_ap(c, 0), scalar2=cw_ap(c, 4),
                                                                          op0=ALU.mult, op1=ALU.add),
                         reads=[xp_b, B_vecT], writes=[ca_b])
                    for tap in range(1, 4):
                        P.op("dve", lambda e, c=c, ca_t=ca_t, tap=tap: e.scalar_tensor_tensor(
                            out=ca_t[:, 0:T], in0=xp_t[:, c, tap:tap + T], scalar=cw_ap(c, tap), in1=ca_t[:, 0:T], op0=ALU.mult, op1=ALU.add),
                            reads=[xp_b, B_vecT, ca_b], writes=[ca_b])
                    s1.append((c, ca_t, ca_b))
                yield
            P.op("dve", lambda e: e.tensor_copy(out=hist[:], in_=xp_t[:, :, T:T + 3]), reads=[xp_b], writes=[B_hist])

        def stage_xsB(G):
            L, nt = G["L"], G["nt"]
            for t in range(nt):
                bk, bkb = ps()
                bkv = bk[:].bitcast(BF16)

                def tr(e, bkv=bkv, t=t):
                    ins = None
                    for c in range(6):
                        ins = e.transpose(out=bkv[0:L, c * 128:(c + 1) * 128], in_=xpost[:, c, t * L:(t + 1) * L], identity=ident_bf[:])
                    return ins
                P.op("pe", tr, reads=[B_xpost, B_idb], writes=[bkb])
                x_t, x_b = xsB[t]
                P.op("act", lambda e, bkv=bkv, x_t=x_t: e.copy(out=x_t[0:L, :], in_=bkv[0:L, 0:768]), reads=[bkb], writes=[x_b])
                yield

        def stage_ssd(G, t):
            L = G["L"]
            has_state = G["seq"]["has_state"]
            cols = slice(t * L, (t + 1) * L)
            d_t, d_b = dtt[t]
            x_t, x_b = xsB[t]
            z_t, z_b = zs[t]
            da_t, da_b = sm8.next()
            P.op("dve", lambda e: e.tensor_tensor(out=da_t[0:L, :], in0=d_t[0:L, :], in1=a_bc[0:L, :], op=ALU.mult), reads=[d_b, B_s8], writes=[da_b])
            P.op("dve", lambda e: e.tensor_tensor(out=rhs1[0:L, :, 0:L], in0=AP(tri_f, 0, [[128, L], [0, 8], [1, L]]),
                                                   in1=bc3(da_t[0:L, :], L), op=ALU.mult),
                 reads=[B_tri, da_b], writes=[B_rhs1])
            yield
            bkc, bkcb = ps()
            P.op("pe", lambda e: e.matmul(bkc[0:L, 0:8], lhsT=tri_f[0:L, 0:L], rhs=da_t[0:L, :], start=True, stop=True),
                 reads=[B_tri, da_b], writes=[bkcb])
            dct_t, dct_b = sm8.next()
            P.op("act", lambda e: e.copy(out=dct_t[0:L, :], in_=bkc[0:L, 0:8]), reads=[bkcb], writes=[dct_b])
            P.op("dve", lambda e: e.tensor_tensor(out=xdt[0:L, :, :], in0=x_t[0:L, 0:512].rearrange("p (h d) -> p h d", h=8),
                                                   in1=bc3(d_t[0:L, :], 64), op=ALU.mult),
                 reads=[x_b, d_b], writes=[B_xdt])
            sdte_t, sdte_b = sm8.next()
            yield
            for hq in range(2):
                bkb_t, bkb_b = ps()
                P.op("pe", lambda e, bkb_t=bkb_t, hq=hq: e.matmul(bkb_t[:, 0:4 * L].rearrange("p (h q) -> p h q", h=4), lhsT=ones_f[0:L, :],
                                                                  rhs=rhs1[0:L, 4 * hq:4 * hq + 4, 0:L], start=True, stop=True),
                     reads=[B_ones, B_rhs1], writes=[bkb_b])
                bcv = bkb_t[:, 0:4 * L].rearrange("p (h q) -> p h q", h=4)
                bkd, bkdb = ps()
                P.op("pe", lambda e, bkd=bkd, hq=hq: e.matmul(bkd[0:L, 0:L], lhsT=xpost[:, 4 + hq, cols], rhs=xpost[:, 6 + hq, cols], start=True, stop=True),
                     reads=[B_xpost], writes=[bkdb])
                yield
                sg_t, sg_b = segf.next()
                P.op("dve", lambda e, bcv=bcv, sg_t=sg_t, hq=hq: e.tensor_tensor(out=sg_t[0:L, :, 0:L], in0=bcv[0:L], in1=bc3(dct_t[0:L, 4 * hq:4 * hq + 4], L), op=ALU.subtract),
                     reads=[bkb_b, dct_b], writes=[sg_b])
                P.op("dve", lambda e, bcv=bcv, hq=hq: e.tensor_tensor(out=sdte_t[0:L, 4 * hq:4 * hq + 4], in0=bcv[0:L, :, L - 1],
                                                                      in1=dct_t[0:L, 4 * hq:4 * hq + 4], op=ALU.subtract),
                     reads=[bkb_b, dct_b], writes=[sdte_b])
                P.op("dve", lambda e, sg_t=sg_t: e.tensor_tensor(out=sg_t[0:L, :, 0:L], in0=sg_t[0:L, :, 0:L],
                                                                  in1=AP(maskneg, 0, [[128, L], [0, 4], [1, L]]), op=ALU.add),
                     reads=[sg_b, B_mneg], writes=[sg_b])
                yield
                P.op("act", lambda e, bcv=bcv, hq=hq: e.activation(out=ebc[:, 4 * hq:4 * hq + 4, 0:L], in_=bcv, func=AF.Exp), reads=[bkb_b], writes=[B_ebc])
                P.op("act", lambda e, bcv=bcv, hq=hq: e.activation(out=cdec[:, 4 * hq:4 * hq + 4], in_=bcv[:, :, L - 1], func=AF.Exp), reads=[bkb_b], writes=[B_cdec])
                dc_t, dc_b = dec.next()
                P.op("act", lambda e, sg_t=sg_t, dc_t=dc_t: e.activation(out=dc_t[0:L, :, 0:L], in_=sg_t[0:L, :, 0:L], func=AF.Exp), reads=[sg_b], writes=[dc_b])
                yield
                P.op("dve", lambda e, dc_t=dc_t, hq=hq, bkd=bkd: e.tensor_tensor(out=Mt[0:L, 4 * hq:4 * hq + 4, 0:L], in0=dc_t[0:L, :, 0:L],
                                                                                 in1=AP(bkd, 0, [[512, L], [0, 4], [1, L]]), op=ALU.mult),
                     reads=[dc_b, bkdb], writes=[B_Mt])
                if has_state:
                    P.op("dve", lambda e, hq=hq: e.tensor_tensor(out=Cp[:, 4 * hq:4 * hq + 4, 0:L], in0=ebc[:, 4 * hq:4 * hq + 4, 0:L],
                                                                  in1=AP(xpost, (6 + hq) * 256 + t * L, [[2048, 128], [0, 4], [1, L]]), op=ALU.mult),
                         reads=[B_ebc, B_xpost], writes=[B_Cp])
                yield
            P.op("act", lambda e: e.activation(out=sdte_t[0:L, :], in_=sdte_t[0:L, :], func=AF.Exp), reads=[sdte_b], writes=[sdte_b])
            P.op("dve", lambda e: e.tensor_tensor(out=xdte[0:L, :, :], in0=xdt[0:L, :, :], in1=bc3(sdte_t[0:L, :], 64), op=ALU.mult),
                 reads=[B_xdt, sdte_b], writes=[B_xdte])
            bky, bkyb = ps()

            def mmy(e):
                ins = None
                for h in range(8):
                    ins = e.matmul(bky[0:L, h * 64:(h + 1) * 64], lhsT=Mt[0:L, h, 0:L], rhs=xdt[0:L, h, :], start=True, stop=(not has_state))
                    if has_state:
                        ins = e.matmul(bky[0:L, h * 64:(h + 1) * 64], lhsT=Cp[:, h, 0:L], rhs=state_bf[:, h * 64:(h + 1) * 64], start=False, stop=True)
                return ins
            P.op("pe", mmy, reads=[B_Mt, B_xdt] + ([B_Cp, B_statebf] if has_state else []), writes=[bkyb])
            yield
            yield
            bks, bksb = ps()

            def mmst(e):
                ins = None
                for h in range(8):
                    g = h // 4
                    ins = e.matmul(bks[:, h * 64:(h + 1) * 64], lhsT=x_t[0:L, 512 + 128 * g:512 + 128 * (g + 1)], rhs=xdte[0:L, h, :], start=True, stop=True)
                return ins
            P.op("pe", mmst, reads=[x_b, B_xdte], writes=[bksb])
            if has_state:
                P.op("dve", lambda e: e.tensor_tensor(out=stmp[:].rearrange("p (h d) -> p h d", h=8), in0=state[:].rearrange("p (h d) -> p h d", h=8),
                                                      in1=bc3(cdec[:, :], 64), op=ALU.mult),
                     reads=[B_state, B_cdec], writes=[B_stmp])
                P.op("dve", lambda e: e.tensor_tensor(out=state[:], in0=stmp[:], in1=bks[:], op=ALU.add), reads=[B_stmp, bksb], writes=[B_state])
            else:
                P.op("dve", lambda e: e.tensor_copy(out=state[:], in_=bks[:]), reads=[bksb], writes=[B_state])
            P.op("act", lambda e: e.copy(out=state_bf[:], in_=state[:]), reads=[B_state], writes=[B_statebf])
            G["seq"]["has_state"] = True
            yield
            y_t, y_b = ytmp.next()
            P.op("dve", lambda e: e.tensor_tensor(out=y_t[0:L, :].rearrange("p (h d) -> p h d", h=8), in0=x_t[0:L, 0:512].rearrange("p (h d) -> p h d", h=8),
                                                   in1=bc3(dsk_bc[0:L, :], 64), op=ALU.mult),
                 reads=[x_b, B_s8], writes=[y_b])
            P.op("dve", lambda e: e.tensor_tensor(out=y_t[0:L, :], in0=y_t[0:L, :], in1=bky[0:L, :], op=ALU.add), reads=[y_b, bkyb], writes=[y_b])
            P.op("dve", lambda e: e.tensor_tensor(out=y_t[0:L, :], in0=y_t[0:L, :], in1=z_t[0:L, :], op=ALU.mult), reads=[y_b, z_b], writes=[y_b])
            yield
            st_t, st_b = stats.next()
            m_t, m_b = mix[t]
            P.op("act", lambda e: e.activation(out=m_t[0:L, 0:512], in_=y_t[0:L, :], func=AF.Square, accum_out=st_t[0:L, 0:1]),
                 reads=[y_b], writes=[m_b, st_b])
            rstd_of(None, st_t, st_b, L, 512)
            P.op("dve", lambda e: e.scalar_tensor_tensor(out=m_t[0:L, 0:512], in0=y_t[0:L, :], scalar=st_t[0:L, 2:3], in1=ssdg_bc[0:L, :],
                                                         op0=ALU.mult, op1=ALU.mult),
                 reads=[y_b, st_b, B_ssdg], writes=[m_b])

        def stage_attn(G, t):
            L = G["L"]
            ti = G["ti0"] + t
            m_t, m_b = mix[t]
            keys = []
            for j in range(-4, 1):
                kt = ti + j
                if kt < 0:
                    continue
                Lk = L if j == 0 else 128
                keys.append((j, kt % 6, Lk))
            res = ATT[t % 2]
            ob = [banks[res["ob"][0]], banks[res["ob"][1]]]
            ebuf, PT = res["ebuf"], res["PT"]
            pend = []

            def emit_pv(item):
                (j, slot, Lk, hq, pt_t, pt_b, first, last) = item
                o_t, o_b = ob[hq]

                def pv(e):
                    ins = None
                    for hh in range(4):
                        h = 4 * hq + hh
                        ins = e.matmul(o_t[0:L, hh * 65:(hh + 1) * 65], lhsT=pt_t[0:Lk, hh, 0:L],
                                       rhs=AP(Vr, slot * 520 + h * 65, [[6 * 520, Lk], [1, 65]]), start=(first and hh == 0), stop=last)
                    return ins
                P.op("pe", pv, reads=[pt_b, B_Vr], writes=[o_b])

            for idx, (j, slot, Lk) in enumerate(keys):
                for hq in range(2):
                    bk, bkb = ps()

                    def qk(e, bk=bk, hq=hq, slot=slot, Lk=Lk):
                        ins = None
                        for hh in range(4):
                            h = 4 * hq + hh
                            hp = h // 2
                            ins = e.matmul(bk[0:Lk, hh * L:(hh + 1) * L], lhsT=Kt[:, hp, slot * 128:slot * 128 + Lk],
                                           rhs=qT[:, h, t * L:(t + 1) * L], start=True, stop=True)
                        return ins
                    P.op("pe", qk, reads=[B_Kt, B_qT], writes=[bkb])
                    e_t, e_b = ebuf.next()
                    P.op("act", lambda e, bk=bk, e_t=e_t, Lk=Lk: e.activation(out=e_t[0:Lk, :, 0:L], in_=bk[0:Lk, 0:4 * L].rearrange("p (h q) -> p h q", h=4), func=AF.Exp),
                         reads=[bkb], writes=[e_b])
                    pt_t, pt_b = PT.next()
                    c0 = -128 * j
                    P.op("dve", lambda e, e_t=e_t, pt_t=pt_t, Lk=Lk, hq=hq, c0=c0: e.tensor_tensor(
                        out=pt_t[0:Lk, :, 0:L], in0=e_t[0:Lk, :, 0:L], in1=tab_e[0:Lk, 4 * hq:4 * hq + 4, c0:c0 + L], op=ALU.mult),
                        reads=[e_b, B_tab], writes=[pt_b])
                    pend.append((j, slot, Lk, hq, pt_t, pt_b, idx == 0, idx == len(keys) - 1))
                    if len(pend) > 2:
                        emit_pv(pend.pop(0))
                    yield
            while pend:
                emit_pv(pend.pop(0))
            for hq in range(2):
                o_t, o_b = ob[hq]
                r_t, r_b = rec.next()
                P.op("dve", lambda e, o_t=o_t, r_t=r_t: e.reciprocal(out=r_t[0:L, :], in_=AP(o_t, 64, [[512, L], [65, 4]])), reads=[o_b], writes=[r_b])
                P.op("dve", lambda e, o_t=o_t, r_t=r_t, hq=hq: e.tensor_tensor(
                    out=m_t[0:L, 512 + 256 * hq:512 + 256 * (hq + 1)].rearrange("p (h d) -> p h d", h=4),
                    in0=AP(o_t, 0, [[512, L], [65, 4], [1, 64]]), in1=bc3(r_t[0:L, :], 64), op=ALU.mult),
                    reads=[o_b, r_b], writes=[m_b])
            yield

        def stage_outproj(G, tiles):
            L, nt = G["L"], G["nt"]
            x_t, x_b = x1g[G["par"]]
            mts = {}
            for t in tiles:
                m_t, m_b = mix[t]
                bk, bkb = ps()
                bkv = bk[:].bitcast(BF16)

                def tr(e, m_t=m_t, bkv=bkv):
                    ins = None
                    for k in range(8):
                        ins = e.transpose(out=bkv[:, k * L:(k + 1) * L], in_=m_t[0:L, k * 128:(k + 1) * 128], identity=ident_bf[0:L, 0:L])
                    return ins
                P.op("pe", tr, reads=[m_b, B_idb], writes=[bkb])
                mt_t, mt_b = mixT.next()
                P.op("act", lambda e, mt_t=mt_t, bkv=bkv: e.copy(out=mt_t[:, :, 0:L], in_=bkv[:, 0:8 * L].rearrange("p (k l) -> p k l", k=8)), reads=[bkb], writes=[mt_b])
                mts[t] = (mt_t, mt_b)
                yield
            for t in tiles:
                mt_t, mt_b = mts[t]
                for half in range(2):
                    bk2, bk2b = ps()

                    def mm(e, bk2=bk2, half=half, mt_t=mt_t):
                        ins = None
                        for k in range(8):
                            ins = e.matmul(bk2[0:L, :], lhsT=mt_t[:, k, 0:L], rhs=wout[:, k, half * 512:(half + 1) * 512], start=(k == 0), stop=(k == 7))
                        return ins
                    P.op("pe", mm, reads=[mt_b, B_wout], writes=[bk2b])
                    P.op("dve", lambda e, bk2=bk2, half=half, t=t: e.tensor_tensor(out=x_t[0:L, t, half * 512:(half + 1) * 512],
                                                                              in0=x_t[0:L, t, half * 512:(half + 1) * 512], in1=bk2[0:L, :], op=ALU.add),
                         reads=[bk2b, x_b], writes=[x_b])
                    yield

        def load_ffn_gu(jb):
            w_t, w_b = wgu[jb % 3]
            P.dma("sp", w_t[:], AP(wbs, jb * 128 * 3072, [[3072, 128], [1, 2048]]), reads=[B_blk_w[jb]], writes=[w_b])

        def load_ffn_dn(jb):
            d_t, d_b = wdn[jb % 3]
            P.dma("sp", d_t[:], AP(wbs, jb * 128 * 3072 + 2048, [[3072, 128], [1, 1024]]), reads=[B_blk_w[jb]], writes=[d_b])

        def stage_ffn(G):
            T, L, nt = G["T"], G["L"], G["nt"]
            x_t, x_b = x1g[G["par"]]
            acc = {}
            for t in range(nt):
                for half in range(2):
                    acc[(t, half)] = banks[t * 2 + half]
            pend = []

            def emit_down(item):
                (j, s, a_t, a_b) = item
                d_t, d_b = wdn[s]

                def mm(e):
                    ins = None
                    for t in range(nt):
                        for half in range(2):
                            ins = e.matmul(acc[(t, half)][0][0:L, :], lhsT=a_t[:, t * L:(t + 1) * L], rhs=d_t[:, half * 512:(half + 1) * 512],
                                           start=(j == 0), stop=(j == NJB - 1))
                    return ins
                P.op("pe", mm, reads=[a_b, d_b], writes=[acc[k][1] for k in acc])
                if j + 3 < NJB:
                    load_ffn_dn(j + 3)

            p_sig, p_mul = [], []
            for jb in range(NJB + 2):
                if p_sig:
                    (j1, s1, bk1, bk1b) = p_sig.pop(0)
                    s_t, s_b = sg.next()
                    act_sigmoid(s_t[:, 0:T], bk1[:, 0:T], [bk1b], s_b)
                    p_mul.append((j1, s1, bk1, bk1b, s_t, s_b))
                if jb < NJB:
                    s = jb % 3
                    w_t, w_b = wgu[s]
                    g_t = w_t[:, 0:1024].rearrange("p (k n) -> p k n", k=8)
                    u_t = w_t[:, 1024:2048].rearrange("p (k n) -> p k n", k=8)
                    bk, bkb = ps()

                    def mm(e, bk=bk, g_t=g_t, u_t=u_t):
                        ins = None
                        for k in range(8):
                            ins = e.matmul(bk[:, 0:T], lhsT=g_t[:, k, :], rhs=fT[:, k, 0:T], start=(k == 0), stop=(k == 7))
                        for k in range(8):
                            ins = e.matmul(bk[:, 256:256 + T], lhsT=u_t[:, k, :], rhs=fT[:, k, 0:T], start=(k == 0), stop=(k == 7))
                        return ins
                    P.op("pe", mm, reads=[w_b, B_fT], writes=[bkb])
                    if jb + 3 < NJB:
                        load_ffn_gu(jb + 3)
                    p_sig.append((jb, s, bk, bkb))
                yield
                if pend:
                    emit_down(pend.pop(0))
                yield
                if p_mul:
                    (j1, s1, bk1, bk1b, s_t, s_b) = p_mul.pop(0)
                    P.op("dve", lambda e, bk1=bk1, s_t=s_t: e.tensor_tensor(out=s_t[:, 0:T], in0=s_t[:, 0:T], in1=bk1[:, 0:T], op=ALU.mult),
                         reads=[s_b, bk1b], writes=[s_b])
                    a_t, a_b = aT.next()
                    P.op("dve", lambda e, bk1=bk1, s_t=s_t, a_t=a_t: e.tensor_tensor(out=a_t[:, 0:T], in0=s_t[:, 0:T], in1=bk1[:, 256:256 + T], op=ALU.mult),
                         reads=[s_b, bk1b], writes=[a_b])
                    pend.append((j1, s1, a_t, a_b))
                yield
            while pend:
                emit_down(pend.pop(0))
            for t in range(nt):
                for half in range(2):
                    a_t, a_b = acc[(t, half)]
                    P.op("dve", lambda e, a_t=a_t, t=t, half=half: e.tensor_tensor(out=x_t[0:L, t, half * 512:(half + 1) * 512],
                                                                                   in0=x_t[0:L, t, half * 512:(half + 1) * 512], in1=a_t[0:L, :], op=ALU.add),
                         reads=[a_b, x_b], writes=[x_b])
                out_toks.append(P.dma("sp", G["y_out"](t), x_t[0:L, t, :], reads=[x_b]))
                yield

        def emit_state_out(dst_tensor, dst_off):
            bk, bkb = ps()

            def tr(e):
                ins = None
                for hp in range(4):
                    ins = e.transpose(out=bk[:, hp * 128:(hp + 1) * 128], in_=state[:, hp * 128:(hp + 1) * 128], identity=ident_f[:])
                return ins
            P.op("pe", tr, reads=[B_state, B_idf], writes=[bkb])
            o_t, o_b = ost.next()
            P.op("act", lambda e: e.copy(out=o_t[:], in_=bk[:]), reads=[bkb], writes=[o_b])
            out_toks.append(P.dma("sp", AP(dst_tensor, dst_off, [[128, 128], [128 * 128, 4], [1, 128]]),
                                  o_t[:].rearrange("p (a n) -> p a n", a=4), reads=[o_b]))

        def emit_conv_out(G, dst_tensor, dst_off):
            T = G["T"]
            xp_t, xp_b = xpre[G["par"]]
            bk, bkb = ps()
            bkv = bk[:].bitcast(BF16)

            def tr(e):
                ins = None
                for c in range(8):
                    ins = e.transpose(out=bkv[0:3, c * 128:(c + 1) * 128], in_=xp_t[:, c, T:T + 3], identity=ident_bf[:])
                return ins
            P.op("pe", tr, reads=[xp_b, B_idb], writes=[bkb])
            for half in range(2):
                o_t, o_b = ost.next()
                P.op("act", lambda e, o_t=o_t, half=half: e.copy(out=o_t[0:3, :], in_=bkv[0:3, half * 512:(half + 1) * 512]), reads=[bkb], writes=[o_b])
                out_toks.append(P.dma("sp", AP(dst_tensor, dst_off + half * 512, [[D, 3], [1, 512]]), o_t[0:3, :], reads=[o_b]))

        def gen_head(G, prev):
            T, L, nt = G["T"], G["L"], G["nt"]
            x_t, x_b = x1g[G["par"]]
            xp_t, xp_b = xpre[G["par"]]
            if G["seq"]["kind"] == "sample":
                yield from gen_sample_init(G)
            elif prev is not None:
                P.op("dve", lambda e: e.tensor_copy(out=xp_t[:, :, 0:3], in_=hist[:]), reads=[B_hist], writes=[xp_b])
            else:
                P.op("dve", lambda e: e.memset(xp_t[:, :, 0:3], 0.0), writes=[xp_b])
            src = [(x_t[0:L, t, :], x_b) for t in range(nt)]
            first = (G["seq"]["kind"] == "prompt" and G["g"] == 0 and G["seq"]["b"] == 0)
            yield from stage_norm_T(src, gmixT, hT, B_hT, L)
            if first:
                dump("hT", hT[:], B_hT)
            yield from stage_inproj(G)
            yield from stage_inproj_tok(G)
            yield from stage_conv(G)
            yield from stage_xsB(G)

        def run_tail(G):
            T, L, nt = G["T"], G["L"], G["nt"]
            x_t, x_b = x1g[G["par"]]
            first = (G["seq"]["kind"] == "prompt" and G["g"] == 0 and G["seq"]["b"] == 0)
            def outnorm(t):
                yield from stage_outproj(G, [t])
                yield from stage_norm_T([(x_t[0:L, t, :], x_b)], gffnT, fT, B_fT, L, t0=t, scratch=[mix[t]])
            if nt > 1:
                drive([[stage_attn(G, 0), PSA(ATT[0]["psa"]), 2,
                        lambda: [[stage_attn(G, 1), PSA(ATT[1]["psa"]), 2], [outnorm(0), PSA([2, 3]), 1]]],
                       [stage_ssd(G, 1), PSA([4, 5]), 2]])
                last_on = outnorm(1)
            else:
                drive([[stage_attn(G, 0), PSA(ATT[0]["psa"]), 1]])
                last_on = outnorm(0)
            return last_on

        def load_x(G):
            x_t, x_b = x1g[G["par"]]
            for t in range(G["nt"]):
                P.dma("sp", x_t[0:G["L"], t, :], G["x_in"](t), writes=[x_b])

        groups = []
        par = 0

        def add_prompt_seq(b):
            nonlocal par
            seq = {"kind": "prompt", "has_state": False, "b": b}
            for g in range(8):
                G = {"seq": seq, "g": g, "T": 256, "L": 128, "nt": 2, "par": par, "ti0": 2 * g, "slot0": (2 * g) % 6,
                     "knf": [], "kv_out": None}
                G["x_in"] = (lambda t, b=b, g=g: AP(xp, (b * SEQ + g * 256 + t * 128) * D, [[D, 128], [1, D]]))
                G["y_out"] = (lambda t, b=b, g=g: AP(yp, (b * SEQ + g * 256 + t * 128) * D, [[D, 128], [1, D]]))
                if g >= 6:
                    G["kv_out"] = ((lambda t, b=b, g=g: AP(kpo, (b * 512 + (g - 6) * 256 + t * 128) * 512, [[512, 128], [1, 512]])),
                                   (lambda t, b=b, g=g: AP(vpo, (b * 512 + (g - 6) * 256 + t * 128) * 512, [[512, 128], [1, 512]])))
                groups.append(G)
                par ^= 1
        add_prompt_seq(0)
        seq_s = {"kind": "sample", "has_state": True, "b": 0}
        Gs = {"seq": seq_s, "g": 0, "T": 32, "L": 32, "nt": 1, "par": par, "ti0": 4, "slot0": 4, "knf": []}
        Gs["x_in"] = (lambda t: AP(xsm, 0, [[D, 32], [1, D]]))
        Gs["y_out"] = (lambda t: AP(ys, 0, [[D, 32], [1, D]]))
        Gs["kv_out"] = ((lambda t: AP(kso, 0, [[512, 32], [1, 512]])), (lambda t: AP(vso, 0, [[512, 32], [1, 512]])))
        groups.append(Gs)
        par ^= 1
        add_prompt_seq(1)

        def gen_sample_init(G):
            xp_t, xp_b = xpre[G["par"]]
            bk, bkb = ps()
            for half in range(2):
                o_t, o_b = ost.next()
                P.dma("sp", o_t[0:3, :], AP(sconv, half * 512, [[D, 3], [1, 512]]), writes=[o_b])

                def trc(e, bk=bk, o_t=o_t, half=half):
                    ins = None
                    for c in range(4):
                        cc = half * 4 + c
                        ins = e.transpose(out=bk[:, cc * 3:(cc + 1) * 3], in_=o_t[0:3, c * 128:(c + 1) * 128], identity=ident_f[0:3, 0:3])
                    return ins
                P.op("pe", trc, reads=[o_b, B_idf], writes=[bkb])
            P.op("act", lambda e, bk=bk, xp_t=xp_t: e.copy(out=xp_t[:, :, 0:3], in_=bk[:, 0:24].rearrange("p (c r) -> p c r", c=8)), reads=[bkb], writes=[xp_b])
            yield
            o_t, o_b = ost.next()
            P.dma("sp", o_t[:].rearrange("p (a n) -> p a n", a=4), AP(sssm, 0, [[128, 128], [128 * 128, 4], [1, 128]]), writes=[o_b])
            bk2, bk2b = ps()

            def trs(e, bk2=bk2, o_t=o_t):
                ins = None
                for hp in range(4):
                    ins = e.transpose(out=bk2[:, hp * 128:(hp + 1) * 128], in_=o_t[:, hp * 128:(hp + 1) * 128], identity=ident_f[:])
                return ins
            P.op("pe", trs, reads=[o_b, B_idf], writes=[bk2b])
            P.op("dve", lambda e, bk2=bk2: e.tensor_copy(out=state[:], in_=bk2[:]), reads=[bk2b], writes=[B_state])
            P.op("act", lambda e: e.copy(out=state_bf[:], in_=state[:]), reads=[B_state], writes=[B_statebf])
            yield
            for kt in range(4):
                P.dma("pool", xpost[:].rearrange("p c t -> p (c t)")[:, kt * 512:(kt + 1) * 512], AP(ck, kt * 128 * 512, [[512, 128], [1, 512]]), writes=[B_xpost])
                P.dma("pool", AP(Vr, kt * 520, [[6 * 520, 128], [65, 8], [1, 64]]), AP(cv, kt * 128 * 512, [[512, 128], [64, 8], [1, 64]]),
                      writes=[B_Vr])
            for hp in range(4):
                bk3, bk3b = ps()
                bkv = bk3[:].bitcast(BF16)

                def trk(e, bkv=bkv, hp=hp):
                    ins = None
                    for kt in range(4):
                        ins = e.transpose(out=bkv[:, kt * 128:(kt + 1) * 128], in_=xpost[:].rearrange("p c t -> p (c t)")[:, kt * 512 + hp * 128:kt * 512 + (hp + 1) * 128], identity=ident_bf[:])
                    return ins
                P.op("pe", trk, reads=[B_xpost, B_idb], writes=[bk3b])
                P.op("act", lambda e, bkv=bkv, hp=hp: e.copy(out=Kt[:, hp, 0:512], in_=bkv[:, 0:512]), reads=[bk3b], writes=[B_Kt])
                yield

        def prev_of(gi):
            if gi == 0 or groups[gi - 1]["seq"] is not groups[gi]["seq"]:
                return None
            return groups[gi - 1]

        try:
          _chk("setup")
          load_x(groups[0])
          drive([[chain(gen_head(groups[0], None), stage_ssd(groups[0], 0), stage_inproj_qk(groups[0])), PSA_ALL, 1]])
          for gi, G in enumerate(groups):
              if gi + 1 < len(groups):
                  load_x(groups[gi + 1])
              for jb0 in range(3):
                  load_ffn_gu(jb0)
                  load_ffn_dn(jb0)
              last_on = run_tail(G)
              last_of_seq = (gi + 1 == len(groups)) or (groups[gi + 1]["seq"] is not G["seq"])
              if last_of_seq:
                  if G["seq"]["kind"] == "prompt":
                      b = G["seq"]["b"]
                      emit_state_out(spo, b * 512 * 128)
                      emit_conv_out(G, cpo, b * 3 * D)
                  else:
                      emit_state_out(sso, 0)
                      emit_conv_out(G, cso, 0)
              items = [[chain(last_on, stage_ffn(G)), PSA([4, 5]), 1]]
              if gi + 1 < len(groups):
                  Gn = groups[gi + 1]
                  items.append([chain(gen_head(Gn, prev_of(gi + 1)), stage_ssd(Gn, 0), stage_inproj_qk(Gn)), PSA([6, 7]), 1])
              drive(items)
              _chk("g%d" % gi)
        except _Stop:
          pass

        P.wait_all("sp", out_toks)
        P.emit()
    return nc


_CACHE = {}


def kernel(x_prompt, x_sample, cache_attn_k, cache_attn_v, state_ssm, state_conv,
           norm_mix_g, w_in, conv_w, conv_b, dt_bias, a_log, d_skip, ssd_norm_g,
           q_norm_g, k_norm_g, rel_bias, w_out, norm_ffn_g, w_gate, w_up, w_down):
    f = lambda a: np.ascontiguousarray(np.asarray(a, dtype=np.float32))
    if "nc" not in _CACHE:
        _CACHE["nc"] = build_program()
    nc = _CACHE["nc"]
    x_prompt = f(x_prompt); x_sample = f(x_sample)
    ckk = f(cache_attn_k)[0].reshape(8, 512, 512)
    cvv = f(cache_attn_v)[0].reshape(8, 512, 512)
    sss = f(state_ssm)[0].reshape(8, 512, 128)
    scv = f(state_conv)[0]
    vecs = np.concatenate([f(conv_w)[0], f(conv_b), f(norm_mix_g), f(norm_ffn_g)], axis=0)
    small8 = np.concatenate([f(dt_bias), f(a_log), f(d_skip)], axis=0)
    qkg = np.concatenate([f(q_norm_g), f(k_norm_g)], axis=0)
    shared = {"w_in": f(w_in)[0], "w_out": f(w_out)[0], "w_gate": f(w_gate)[0], "w_up": f(w_up)[0], "w_down": f(w_down)[0],
              "vecs": f(vecs), "small8": f(small8), "ssdg": f(ssd_norm_g), "qkg": f(qkg), "relb": f(rel_bias)[0]}
    in_maps = []
    for c in range(NCORES):
        m = dict(shared)
        m.update({"xp": x_prompt[2 * c:2 * c + 2], "xsm": x_sample[c], "ck": ckk[c], "cv": cvv[c], "sssm": sss[c], "sconv": scv[c]})
        in_maps.append(m)
    res = run_bass_kernel_spmd(nc, in_maps, core_ids=list(range(NCORES)))
    R = res.results
    if DBG["dump"]:
        DBG["vals"] = {n: np.asarray(R[0][n]) for n in DBG["names"]}
    cat = lambda k: np.concatenate([np.asarray(r[k]) for r in R], axis=0)
    y_prompt = cat("yp").reshape(16, SEQ, D)
    y_sample = np.stack([np.asarray(r["ys"]) for r in R], axis=0)
    k_prompt = cat("kpo").reshape(1, 16, 512, 8, 64)
    v_prompt = cat("vpo").reshape(1, 16, 512, 8, 64)
    ssm_prompt = cat("spo").reshape(1, 16, 8, 64, 128)
    conv_prompt = cat("cpo").reshape(1, 16, 3, D)
    k_sample = np.stack([np.asarray(r["kso"]) for r in R], axis=0).reshape(1, 8, 32, 8, 64)
    v_sample = np.stack([np.asarray(r["vso"]) for r in R], axis=0).reshape(1, 8, 32, 8, 64)
    ssm_sample = np.stack([np.asarray(r["sso"]) for r in R], axis=0).reshape(1, 8, 8, 64, 128)
    conv_sample = np.stack([np.asarray(r["cso"]) for r in R], axis=0).reshape(1, 8, 3, D)
    return tuple(np.ascontiguousarray(a, dtype=np.float32) for a in
                 (y_prompt, y_sample, k_prompt, v_prompt, ssm_prompt, conv_prompt, k_sample, v_sample, ssm_sample, conv_sample))
```
